# Optimizing a Trainium2 kernel written in Bass

```python
import jax, jax.numpy as jnp
from jax import lax
import numpy as np

D_MODEL = 2048
BATCH = 16
SEQ = 2048
DEPTH = 1
DEC_BATCH = 2
DEC_SEQ = 8192
PAST_LEN = 128

HEAD_DIM = 128
N_Q_HEADS = 8
N_KV_HEADS = 2
GQA_GROUP = N_Q_HEADS // N_KV_HEADS
D_ATTN = N_Q_HEADS * HEAD_DIM
D_KV = N_KV_HEADS * HEAD_DIM
WINDOW = 128
BLOCK = 128
D_CONV = D_MODEL // 4
CONV_WIDTH = 3
N_MEM = 256
N_X_HEADS = 4
D_XATTN = N_X_HEADS * HEAD_DIM
D_MIX = D_ATTN + D_CONV + D_XATTN
IN_SIZES = (D_ATTN, D_KV, D_KV, D_CONV, D_CONV, D_CONV, D_XATTN)
D_IN = sum(IN_SIZES)
N_BRANCH = 3
D_FF = 5504
N_BUCKETS = 32
MAX_DISTANCE = 128
LN_EPS = 1e-5
NEG_INF = -1e30
ALPHA = (2 * DEPTH) ** 0.25
BETA = (8 * DEPTH) ** -0.25

kernel_name = "hybrid_gated_window_conv_memory_encoder"


def _band_geometry():
    t = np.arange(BLOCK)[:, None]
    j = np.arange(3 * BLOCK)[None, :]
    rel = j - BLOCK - t
    half = N_BUCKETS // 2
    max_exact = half // 2
    n = np.abs(rel)
    large = max_exact + (np.log(np.maximum(n, 1) / max_exact) / np.log(MAX_DISTANCE / max_exact) * (half - max_exact)).astype(np.int32)
    large = np.minimum(large, half - 1)
    bucket = (rel > 0).astype(np.int32) * half + np.where(n < max_exact, n, large)
    return rel, bucket.astype(np.int32)


def _layer_norm(x, g, b):
    xf = x.astype(jnp.float32)
    mu = xf.mean(-1, keepdims=True)
    var = jnp.square(xf - mu).mean(-1, keepdims=True)
    return ((xf - mu) * lax.rsqrt(var + LN_EPS) * g.astype(jnp.float32) + b.astype(jnp.float32)).astype(x.dtype)


def _swiglu(x, w_gate_up, w_down):
    g, u = jnp.split(x @ w_gate_up, 2, axis=-1)
    return (jax.nn.silu(g) * u) @ w_down


def _window_attention(q, k, v, rel_bias, sink):
    B, S = q.shape[:2]
    n = S // BLOCK
    qb = q.reshape(B, n, BLOCK, N_KV_HEADS, GQA_GROUP, HEAD_DIM)

    def band(t):
        tp = jnp.pad(t, ((0, 0), (BLOCK, BLOCK), (0, 0), (0, 0))).reshape(B, n + 2, BLOCK, N_KV_HEADS, HEAD_DIM)
        return jnp.concatenate([tp[:, :-2], tp[:, 1:-1], tp[:, 2:]], axis=2)

    kb, vb = band(k), band(v)
    s = jnp.einsum('bnqhgd,bnkhd->bnhgqk', qb, kb, preferred_element_type=jnp.float32) * (HEAD_DIM ** -0.5)
    s = s + rel_bias.reshape(1, 1, N_KV_HEADS, GQA_GROUP, BLOCK, 3 * BLOCK)
    rel, _ = _band_geometry()
    kpos = (np.arange(n)[:, None] - 1) * BLOCK + np.arange(3 * BLOCK)[None, :]
    valid = (np.abs(rel) <= WINDOW)[None] & ((kpos >= 0) & (kpos < S))[:, None, :]
    s = jnp.where(valid[None, :, None, None], s, NEG_INF)
    sink_l = sink.astype(jnp.float32).reshape(1, 1, N_KV_HEADS, GQA_GROUP, 1, 1)
    m = jnp.maximum(s.max(-1, keepdims=True), sink_l)
    p = jnp.exp(s - m)
    p = (p / (p.sum(-1, keepdims=True) + jnp.exp(sink_l - m))).astype(v.dtype)
    o = jnp.einsum('bnhgqk,bnkhd->bnqhgd', p, vb)
    return o.reshape(B, S, D_ATTN)


def _short_conv(u, w):
    up = jnp.pad(u, ((0, 0), (1, 1), (0, 0)))
    return up[:, :-2] * w[0] + up[:, 1:-1] * w[1] + up[:, 2:] * w[2]


def _memory_attention(q, mem, w_mem_kv):
    B, S = q.shape[:2]
    kv = (mem @ w_mem_kv).reshape(B, N_MEM, 2, N_X_HEADS, HEAD_DIM)
    s = jnp.einsum('bshd,bmhd->bhsm', q, kv[:, :, 0], preferred_element_type=jnp.float32) * (HEAD_DIM ** -0.5)
    p = jax.nn.softmax(s, axis=-1).astype(q.dtype)
    return jnp.einsum('bhsm,bmhd->bshd', p, kv[:, :, 1]).reshape(B, S, D_XATTN)


def _layer(x, mem, rel_bias, ln1_g, ln1_b, ffn1_w_gate_up, ffn1_w_down, w_in, conv_w, w_mem_kv,
           attn_sink, w_gate, b_gate, w_branch, w_o, ln2_g, ln2_b, ffn2_w_gate_up, ffn2_w_down, ln3_g, ln3_b):
    B, S, _ = x.shape
    x = _layer_norm(ALPHA * x + 0.5 * _swiglu(x, ffn1_w_gate_up, ffn1_w_down), ln1_g, ln1_b)
    cuts = list(np.cumsum(IN_SIZES)[:-1])
    q, k, v, cb, cc, ch, qx = jnp.split(x @ w_in, cuts, axis=-1)
    y_attn = _window_attention(q.reshape(B, S, N_Q_HEADS, HEAD_DIM), k.reshape(B, S, N_KV_HEADS, HEAD_DIM),
                               v.reshape(B, S, N_KV_HEADS, HEAD_DIM), rel_bias, attn_sink)
    y_conv = cb * _short_conv(cc * ch, conv_w)
    y_mem = _memory_attention(qx.reshape(B, S, N_X_HEADS, HEAD_DIM), mem, w_mem_kv)
    gates = jax.nn.sigmoid(x @ w_gate + b_gate).reshape(B, S, N_BRANCH, D_MODEL)
    merged = (gates[:, :, 0] * (y_attn @ w_branch[:D_ATTN])
              + gates[:, :, 1] * (y_conv @ w_branch[D_ATTN:D_ATTN + D_CONV])
              + gates[:, :, 2] * (y_mem @ w_branch[D_ATTN + D_CONV:]))
    x = _layer_norm(ALPHA * x + merged @ w_o, ln2_g, ln2_b)
    x = _layer_norm(ALPHA * x + 0.5 * _swiglu(x, ffn2_w_gate_up, ffn2_w_down), ln3_g, ln3_b)
    return x


def _forward(x, mem, rel_bias, ln1_g, ln1_b, ffn1_w_gate_up, ffn1_w_down, w_in, conv_w, w_mem_kv,
             attn_sink, w_gate, b_gate, w_branch, w_o, ln2_g, ln2_b, ffn2_w_gate_up, ffn2_w_down, ln3_g, ln3_b):
    for l in range(DEPTH):
        x = _layer(x, mem, rel_bias, ln1_g[l], ln1_b[l], ffn1_w_gate_up[l], ffn1_w_down[l], w_in[l], conv_w[l],
                   w_mem_kv[l], attn_sink[l], w_gate[l], b_gate[l], w_branch[l], w_o[l], ln2_g[l], ln2_b[l],
                   ffn2_w_gate_up[l], ffn2_w_down[l], ln3_g[l], ln3_b[l])
    return x


def setup_inputs(seed: int = 0) -> dict:
    key = jax.random.key(seed)
    ks = jax.random.split(key, 24)
    f32 = jnp.float32
    nrm = lambda k, shape, s: jax.random.normal(k, shape, f32) * s
    L = DEPTH
    return {
        "x_prompt": nrm(ks[0], (BATCH, SEQ, D_MODEL), 1.0),
        "x_sample": nrm(ks[1], (DEC_BATCH, DEC_SEQ, D_MODEL), 1.0),
        "mem_prompt": nrm(ks[2], (BATCH, N_MEM, D_MODEL), 1.0),
        "mem_sample": nrm(ks[3], (DEC_BATCH, N_MEM, D_MODEL), 1.0),
        "rel_bias_table": nrm(ks[4], (N_BUCKETS, N_Q_HEADS), 0.5),
        "ln1_g": 1.0 + nrm(ks[5], (L, D_MODEL), 0.02),
        "ln1_b": nrm(ks[6], (L, D_MODEL), 0.02),
        "ffn1_w_gate_up": nrm(ks[7], (L, D_MODEL, 2 * D_FF), D_MODEL ** -0.5),
        "ffn1_w_down": nrm(ks[8], (L, D_FF, D_MODEL), BETA * D_FF ** -0.5),
        "w_in": nrm(ks[9], (L, D_MODEL, D_IN), D_MODEL ** -0.5),
        "conv_w": nrm(ks[10], (L, CONV_WIDTH, D_CONV), CONV_WIDTH ** -0.5),
        "w_mem_kv": nrm(ks[11], (L, D_MODEL, 2 * D_XATTN), D_MODEL ** -0.5),
        "attn_sink": nrm(ks[12], (L, N_Q_HEADS), 0.5),
        "w_gate": nrm(ks[13], (L, D_MODEL, N_BRANCH * D_MODEL), D_MODEL ** -0.5),
        "b_gate": nrm(ks[14], (L, N_BRANCH * D_MODEL), 0.1),
        "w_branch": nrm(ks[15], (L, D_MIX, D_MODEL), BETA * D_MIX ** -0.5),
        "w_o": nrm(ks[16], (L, D_MODEL, D_MODEL), BETA * D_MODEL ** -0.5),
        "ln2_g": 1.0 + nrm(ks[17], (L, D_MODEL), 0.02),
        "ln2_b": nrm(ks[18], (L, D_MODEL), 0.02),
        "ffn2_w_gate_up": nrm(ks[19], (L, D_MODEL, 2 * D_FF), D_MODEL ** -0.5),
        "ffn2_w_down": nrm(ks[20], (L, D_FF, D_MODEL), BETA * D_FF ** -0.5),
        "ln3_g": 1.0 + nrm(ks[21], (L, D_MODEL), 0.02),
        "ln3_b": nrm(ks[22], (L, D_MODEL), 0.02),
    }


def reference(x_prompt, x_sample, mem_prompt, mem_sample, rel_bias_table, ln1_g, ln1_b, ffn1_w_gate_up,
              ffn1_w_down, w_in, conv_w, w_mem_kv, attn_sink, w_gate, b_gate, w_branch, w_o, ln2_g, ln2_b,
              ffn2_w_gate_up, ffn2_w_down, ln3_g, ln3_b):
    _, bucket = _band_geometry()
    rel_bias = jnp.transpose(rel_bias_table[bucket], (2, 0, 1)).astype(jnp.float32)
    weights = (ln1_g, ln1_b, ffn1_w_gate_up, ffn1_w_down, w_in, conv_w, w_mem_kv, attn_sink, w_gate, b_gate,
               w_branch, w_o, ln2_g, ln2_b, ffn2_w_gate_up, ffn2_w_down, ln3_g, ln3_b)
    y_prompt = _forward(x_prompt, mem_prompt, rel_bias, *weights)
    y_sample = _forward(x_sample, mem_sample, rel_bias, *weights)
    return (y_prompt, y_sample)
```

```python
from contextlib import ExitStack
import numpy as np
import concourse.bass as bass
import concourse.mybir as mybir
from concourse.bass_utils import run_bass_kernel_spmd

F32 = mybir.dt.float32
BF16 = mybir.dt.bfloat16
AF = mybir.ActivationFunctionType
ALU = mybir.AluOpType

D = 2048
KC = 16
HD = 128
NMEM = 256
ALPHA = 2.0 ** 0.25
SCALE = 128.0 ** -0.5
EPS = 1e-5
NEG = -30000.0
NSLOT = 6
NXN = 3
UL = 2048
CW = 1024
NCV = 8
ENG = ('pe', 'act', 'dve', 'pool', 'sp')


class Cfg:
    def __init__(self, FF=5504, NBS=16, NSEQ=3, HALO=True):
        self.FF = FF
        self.FC = FF // 128
        self.NBS = NBS
        self.NG = NBS // 4
        self.NSEQ = NSEQ
        self.HALO = HALO
        self.NKG = (self.FC + 3) // 4


def _stat(W, col0, ncol=128):
    kc = W.shape[0] // 128
    return np.ascontiguousarray(W[:, col0:col0 + ncol].reshape(kc, 128, ncol).transpose(1, 0, 2)).reshape(128, kc * ncol)


def _mov(W, k0, nk, col0, ncols):
    return np.ascontiguousarray(W[k0 * 128:(k0 + nk) * 128, col0:col0 + ncols].reshape(nk, 128, ncols).transpose(1, 0, 2)).reshape(128, nk * ncols)


def unit_list(cfg):
    FC, NKG = cfg.FC, cfg.NKG
    L = []
    for h in range(4):
        L.append((('mk', h), 2048))
    for i in range(4):
        L.append((('mv', i), 2048))
    for l in (1, 2):
        if l == 2:
            for g in range(2):
                L.append((('k', g), 2048))
            for i in range(2):
                L.append((('v', i), 2048))
            for c in range(4):
                L.append((('cc', c), 2048))
                L.append((('ch', c), 2048))
            for h in range(8):
                L.append((('q', h), 2048))
            for h in range(4):
                L.append((('qx', h), 2048))
            for c in range(4):
                L.append((('cb', c), 2048))
            for m in range(16):
                L.append((('wb', m), 2048))
                for br in range(3):
                    L.append((('wg', m, br), 2048))
            for c in range(4):
                for kg in range(4):
                    L.append((('wo', c, kg), 2048))
        for j in range(FC):
            L.append((('g', l, j), 2048))
            L.append((('u', l, j), 2048))
        for c in range(4):
            for kg in range(NKG):
                nk = min(4, FC - 4 * kg)
                L.append((('dn', l, c, kg), nk * 512))
    return L


def unit_offsets(cfg):
    offs = {}
    o = 0
    for key, ln in unit_list(cfg):
        offs[key] = (o, ln)
        o += ln
    tot = ((o + CW - 1) // CW) * CW
    return offs, tot


def pack_weights(cfg, ffn1_w_gate_up, ffn1_w_down, w_in, w_mem_kv, w_gate, w_branch, w_o,
                 ffn2_w_gate_up, ffn2_w_down):
    offs, tot = unit_offsets(cfg)
    FF, FC = cfg.FF, cfg.FC
    out = np.zeros((128, tot), np.float32)
    wgu = {1: ffn1_w_gate_up, 2: ffn2_w_gate_up}
    wdn = {1: ffn1_w_down, 2: ffn2_w_down}
    for key, (o, ln) in offs.items():
        t = key[0]
        if t == 'mk':
            a = _stat(w_mem_kv, key[1] * 128)
        elif t == 'mv':
            a = _mov(w_mem_kv, key[1] * 4, 4, 512, 512)
        elif t == 'k':
            a = _stat(w_in, 1024 + key[1] * 128)
        elif t == 'v':
            a = _mov(w_in, key[1] * 8, 8, 1280, 256)
        elif t == 'cc':
            a = _stat(w_in, 2048 + key[1] * 128)
        elif t == 'ch':
            a = _stat(w_in, 2560 + key[1] * 128)
        elif t == 'q':
            a = _stat(w_in, key[1] * 128)
        elif t == 'qx':
            a = _stat(w_in, 3072 + key[1] * 128)
        elif t == 'cb':
            a = _stat(w_in, 1536 + key[1] * 128)
        elif t == 'wb':
            a = _stat(w_branch, key[1] * 128)
        elif t == 'wg':
            a = _stat(w_gate, key[2] * D + key[1] * 128)
        elif t == 'wo':
            a = _mov(w_o, key[2] * 4, 4, key[1] * 512, 512)
        elif t == 'g':
            a = _stat(wgu[key[1]], key[2] * 128)
        elif t == 'u':
            a = _stat(wgu[key[1]], FF + key[2] * 128)
        elif t == 'dn':
            nk = min(4, FC - 4 * key[3])
            a = _mov(wdn[key[1]], key[3] * 4, nk, key[2] * 512, 512)
        else:
            raise KeyError(key)
        assert a.shape == (128, ln), (key, a.shape, ln)
        out[:, o:o + ln] = a
    return out


def bias_onehot():
    half = 16
    max_exact = 8
    oh = np.zeros((33, 3 * 255), np.float32)
    for blk in range(3):
        for m in range(255):
            d = m if m <= 127 else m - 255
            rel = -d + 128 * (blk - 1)
            n = abs(rel)
            if n > 128:
                oh[32, blk * 255 + m] = NEG
                continue
            large = max_exact + (np.log(np.maximum(n, 1) / max_exact) / np.log(128 / max_exact) * (half - max_exact)).astype(np.int32)
            large = min(int(large), half - 1)
            bucket = int(rel > 0) * half + (n if n < max_exact else large)
            oh[bucket, blk * 255 + m] = 1.0
    return oh


class Sched:
    def __init__(self):
        self.ops = {e: [] for e in ENG}
        self.cnt = {}
        self.waited = {e: {} for e in ENG}

    def wait(self, eng, deps):
        for d in deps:
            if d is None:
                continue
            k, v = d
            if self.waited[eng].get(k, 0) >= v:
                continue
            self.waited[eng][k] = v
            self.ops[eng].append(('w', k, v))

    def op(self, eng, fn, deps=(), inc=True):
        self.wait(eng, deps)
        if inc:
            c = self.cnt.get(eng, 0) + 1
            self.cnt[eng] = c
            self.ops[eng].append(('i', fn, eng, 1))
            return (eng, c)
        self.ops[eng].append(('n', fn))
        return None

    def dma(self, eng, sem, fn, deps=()):
        self.wait(eng, deps)
        c = self.cnt.get(sem, 0) + 16
        self.cnt[sem] = c
        self.ops[eng].append(('i', fn, sem, 16))
        return (sem, c)


class Banks:
    def __init__(self, n):
        self.n = n
        self.nxt = 0
        self.free = [[] for _ in range(n)]

    def get(self):
        b = self.nxt
        self.nxt = (b + 1) % self.n
        return b, list(self.free[b])

    def release(self, b, toks):
        self.free[b] = list(toks)


def f_mm(out, lhsT, rhs, start, stop):
    return lambda e: e.matmul(out, lhsT=lhsT, rhs=rhs, start=start, stop=stop)


def f_tr(out, in_, ident):
    return lambda e: e.transpose(out, in_, ident)


def f_act(out, in_, func, bias=None, scale=None):
    kw = {}
    if bias is not None:
        kw['bias'] = bias
    if scale is not None:
        kw['scale'] = scale
    return lambda e: e.activation(out=out, in_=in_, func=func, **kw)


def f_stt(out, in0, scalar, in1, op0, op1):
    return lambda e: e.scalar_tensor_tensor(out=out, in0=in0, scalar=scalar, in1=in1, op0=op0, op1=op1)


def f_ts(out, in0, s1, s2, op0, op1=None):
    if op1 is None:
        return lambda e: e.tensor_scalar(out=out, in0=in0, scalar1=s1, scalar2=None, op0=op0)
    return lambda e: e.tensor_scalar(out=out, in0=in0, scalar1=s1, scalar2=s2, op0=op0, op1=op1)


def f_tt(out, in0, in1, op):
    return lambda e: e.tensor_tensor(out=out, in0=in0, in1=in1, op=op)


def f_copy(out, in_):
    return lambda e: e.tensor_copy(out=out, in_=in_)


def f_dma(out, in_):
    return lambda e: e.dma_start(out=out, in_=in_)


def f_memset(ap, v):
    return lambda e: e.memset(ap, v)


def build_program(cfg):
    FC, NBS, NG, NSEQ, NKG = cfg.FC, cfg.NBS, cfg.NG, cfg.NSEQ, cfg.NKG
    offs, TOT = unit_offsets(cfg)
    NCH = TOT // CW
    SEQT = NBS * 128
    nc = bass.Bass("TRN2", target_bir_lowering=False)

    def din(name, shape, dt=F32):
        return nc.dram_tensor(name, shape, dt, kind="ExternalInput").ap()

    xs = din("xs", [NSEQ * SEQT, D])
    xh = din("xh", [256, D])
    mem = din("mem", [NSEQ * NMEM, D])
    wflat = din("wflat", [128, TOT])
    lnp = din("lnp", [6, D])
    bgate = din("bgate", [128, 48])
    convw = din("convw", [128, 12])
    sink = din("sink", [1, 8])
    table = din("table", [32, 8])
    oh = din("oh", [33, 765])
    edge = din("edge", [128, 4])
    identin = din("identin", [128, 128])
    ys = nc.dram_tensor("ys", [NSEQ * SEQT, D], F32, kind="ExternalOutput").ap()
    wscr = nc.dram_tensor("wscr", [128, TOT], BF16, kind="Internal").ap()
    x1scr = nc.dram_tensor("x1scr", [SEQT, D], F32, kind="Internal").ap()
    gsc = nc.dram_tensor("gsc", [24 * 255], F32, kind="Internal").ap()
    zsc = nc.dram_tensor("zsc", [24 * 128 * 255], F32, kind="Internal").ap()

    S = Sched()
    banks = Banks(7)
    es = ExitStack()

    def sb(name, shape, dt):
        return es.enter_context(nc.sbuf_tensor(name, shape, dt))

    with es:
        KT = sb("KT", [128, 2, (NBS + 2) * 128], BF16)
        V = sb("V", [128, NBS + 2, 256], BF16)
        U = sb("U", [128, 4, SEQT + 2], BF16)
        mKT = sb("mKT", [128, 4, NMEM], BF16)
        mV = sb("mV", [128, 2, 512], BF16)
        bhl = sb("bhl", [128, 2, 3, 8, 128], BF16)
        lng = sb("lng", [128, D], F32)
        lnb = sb("lnb", [128, D], F32)
        esk = sb("esk", [128, 8], F32)
        ident = sb("ident", [128, 128], BF16)
        ones = sb("ones", [128, 128], BF16)
        bgt = sb("bgt", [128, 48], F32)
        cw = sb("cw", [128, 12], F32)
        edg = sb("edg", [128, 4], F32)
        ring = sb("ring", [128, NSLOT, UL], BF16)
        acc = sb("acc", [128, 4, D], F32)
        xn = sb("xn", [128, NXN, D], BF16)
        xT = sb("xT", [128, KC, 512], BF16)
        work = sb("work", [128, 25600], BF16)
        sg = sb("sg", [128, 2, 512], F32)
        stt = sb("stt", [128, 4, 24], F32)
        mv = sb("mv", [128, 4, 2], F32)
        rs = sb("rs", [128, 4, 2], F32)
        psf = es.enter_context(nc.psum_tensor("psf", [128, 7, 512], F32))
        pst = es.enter_context(nc.psum_tensor("pst", [128, 1024], BF16))

        def wv(b0, b1, dt, inner=None):
            v = work[:, b0 // 2:b1 // 2]
            if dt == F32:
                v = v.bitcast(F32)
            if inner:
                v = v.rearrange("p (a b) -> p a b", b=inner)
            return v

        tab = work[0:33, 0:16].bitcast(F32)
        ohs = work[0:33, 16:16 + 1530].bitcast(F32)
        gv = work[0:8, 2048:2048 + 1530].bitcast(F32)
        idf = work[:, 4096:4096 + 256].bitcast(F32)
        hT = wv(0, FC * 1024, BF16, 512)
        qT = wv(0, 8192, BF16, 512)
        qxT = wv(8192, 12288, BF16, 512)
        cu = wv(12288, 20480, F32, 512)
        merged = wv(0, 16384, BF16, 512)
        PT = wv(20480, 26624, BF16, 512)
        stmp = wv(26624, 30720, F32, 512)
        dn = wv(30720, 34816, F32, 512)
        sgt = wv(20480, 24576, F32, 512)
        tac = wv(24576, 28672, F32, 512)
        tm2 = wv(28672, 32768, F32, 512)
        ycat = wv(34816, 51200, BF16, 512)

        class St:
            pass
        st = St()
        st.nunits = 0
        st.rel = {}
        st.xn_free = [None] * NXN
        st.pref = None
        st.pre_xr = None
        st.conv_todo = []
        st.xn_i = 0
        st.st_tok = {}
        st.accf = {}
        st.pst_free = None
        st.sg_free = [None, None]
        st.acc_free = []
        st.store_tok = None
        st.x1_store = None
        st.ln_tok = None
        st.cnt2 = 0

        def conv_token(col_end):
            ci = (col_end - 1) // CW
            return ('cv%d' % (ci % NCV), 16 * (ci // NCV + 1))

        def use_unit(key):
            off, L = offs[key]
            n = st.nunits
            st.nunits += 1
            slot = n % NSLOT
            deps = [conv_token(off + L)]
            if n >= NSLOT:
                deps.append(st.rel[n - NSLOT])
            tok = S.dma('sp', 'wl%d' % slot, f_dma(ring[:, slot, 0:L], wscr[:, off:off + L]), deps)
            return slot, tok, n

        def mm(out, lhsT, rhs, start, stop, deps=(), inc=False):
            return S.op('pe', f_mm(out, lhsT, rhs, start, stop), deps, inc)

        def barrier(full=False):
            if not full:
                return
            toks = [(e, S.cnt[e]) for e in ('pe', 'act', 'dve') if S.cnt.get(e, 0) > 0]
            for e in ('pe', 'act', 'dve'):
                S.wait(e, toks)

        def conversions(c0, c1, nfl=NCV):
            for ci in range(c0, c1):
                sem = 'cv%d' % (ci % NCV)
                deps = []
                cp = ci - nfl
                if cp >= 0:
                    deps.append(('cv%d' % (cp % NCV), 16 * (cp // NCV + 1)))
                S.dma('pool', sem, f_dma(wscr[:, ci * CW:(ci + 1) * CW], wflat[:, ci * CW:(ci + 1) * CW]), deps)

        def conv_slice(all_=False):
            if st.conv_rest is None:
                return
            c0, c1 = st.conv_rest
            n = (c1 - c0) if all_ else min(c1 - c0, st.conv_per)
            if all_:
                conversions(c0, c0 + n, 2)
            else:
                st.conv_todo = list(range(c0, c0 + n))
            st.conv_rest = (c0 + n, c1) if c0 + n < c1 else None

        def conv_paced(j, pace_tok):
            todo = st.conv_todo
            if not todo:
                return
            k = -(-len(todo) // max(1, FC - j))
            for _ in range(k):
                ci = todo.pop(0)
                sem = 'cv%d' % (ci % NCV)
                deps = [pace_tok]
                cp = ci - NCV
                if cp >= 0:
                    deps.append(('cv%d' % (cp % NCV), 16 * (cp // NCV + 1)))
                S.dma('pool', sem, f_dma(wscr[:, ci * CW:(ci + 1) * CW], wflat[:, ci * CW:(ci + 1) * CW]), deps)

        def load_acc(src, r0, a0, nb):
            toks = []
            for q in range(nb):
                a = a0 + q
                deps = [st.st_tok.get(a), st.accf.get(a)]
                toks.append(S.dma('pool', 'ld%d' % a, f_dma(acc[:, a, :], src[r0 + q * 128:r0 + (q + 1) * 128, :]), deps))
            return toks

        def load_ln(li):
            deps = [st.ln_free] if getattr(st, 'ln_free', None) else []
            t1 = S.dma('sp', 'lnl', f_dma(lng[:, :], bass.AP(lnp.tensor, 2 * li * D, [[0, 128], [1, D]])), deps)
            t2 = S.dma('sp', 'lnl', f_dma(lnb[:, :], bass.AP(lnp.tensor, (2 * li + 1) * D, [[0, 128], [1, D]])), deps)
            st.ln_tok = t2

        def xn_alloc():
            xi = st.xn_i % NXN
            st.xn_i += 1
            return xi

        def xn_from_acc(a, ready):
            xi = xn_alloc()
            tok = S.op('act', lambda e, a=a, xi=xi: e.copy(out=xn[:, xi, :], in_=acc[:, a, :]), deps=[ready, st.xn_free[xi]])
            st.accf[a] = tok
            return (xi, tok)

        def xn_from_dram(src, row0, deps):
            xi = xn_alloc()
            tok = S.dma('pool', 'xn%d' % xi, f_dma(xn[:, xi, :], src[row0:row0 + 128, :]), [st.xn_free[xi]] + list(deps))
            return (xi, tok)

        def transpose_blk(i, item):
            xi, tok = item
            tk = None
            t_ev = None
            for half in range(2):
                for k8 in range(8):
                    kc = half * 8 + k8
                    tk = S.op('pe', f_tr(pst[:, k8 * 128:(k8 + 1) * 128], xn[:, xi, kc * 128:(kc + 1) * 128], ident[:, :]),
                              deps=[tok, st.pst_free] if k8 == 0 else (), inc=(k8 == 7))
                t_ev = S.op('dve', f_copy(xT[:, half * 8:(half + 1) * 8, i * 128:(i + 1) * 128],
                                          pst[:, :].rearrange("p (k t) -> p k t", t=128)), deps=[tk])
                st.pst_free = t_ev
            st.xn_free[xi] = tk
            return t_ev

        def make_xT(ablks, ready):
            t_ev = None
            for i, a in enumerate(ablks):
                t_ev = transpose_blk(i, xn_from_acc(a, ready[i]))
            return [t_ev]

        def make_xT_items(items, src, row0, deps, nblk, fallback=None):
            items = list(items)
            t_ev = None
            for i in range(nblk):
                if i >= len(items):
                    if fallback is not None:
                        items.append(xn_from_acc(i, fallback[i]))
                    else:
                        items.append(xn_from_dram(src, row0 + i * 128, deps))
                t_ev = transpose_blk(i, items[i])
            return [t_ev]

        def ln_block(i, a, ready, after=None, want_xn=False):
            t = S.op('dve', (lambda e, i=i: e.bn_aggr(mv[:, i, :], stt[:, i, :])), deps=list(ready))
            t = S.op('dve', f_ts(rs[:, i, 0:1], mv[:, i, 1:2], EPS, None, ALU.add), deps=[t])
            t = S.op('act', f_act(rs[:, i, 0:1], rs[:, i, 0:1], AF.Sqrt), deps=[t])
            t = S.op('dve', (lambda e, i=i: e.reciprocal(rs[:, i, 0:1], rs[:, i, 0:1])), deps=[t])
            t = S.op('dve', f_stt(rs[:, i, 1:2], mv[:, i, 0:1], -1.0, rs[:, i, 0:1], ALU.mult, ALU.mult), deps=[t])
            tn = S.op('act', f_act(acc[:, a, :], acc[:, a, :], AF.Identity, bias=rs[:, i, 1:2], scale=rs[:, i, 0:1]), deps=[t])
            tg = S.op('dve', f_tt(acc[:, a, :], acc[:, a, :], lng[:, :], ALU.mult), deps=[tn, st.ln_tok])
            item = None
            if want_xn:
                xi = xn_alloc()
                txn = S.op('dve', f_tt(xn[:, xi, :], acc[:, a, :], lnb[:, :], ALU.add), deps=[tg, st.ln_tok, st.xn_free[xi]])
                item = (xi, txn)
                return None, item
            else:
                tb = S.op('dve', f_tt(acc[:, a, :], acc[:, a, :], lnb[:, :], ALU.add), deps=[tg, st.ln_tok])
            if after is not None:
                after(i, a, tb)
            st.ln_free = tb
            st.accf[a] = tb
            return tb, item

        def store_blk(dst, r0, a, dep):
            tok = S.dma('pool', 'st%d' % a, f_dma(dst[r0:r0 + 128, :], acc[:, a, :]), [dep])
            st.st_tok[a] = tok
            return tok

        def ffn_gateup(l, N, xT_ready, hook=None):
            tH = None
            for j in range(FC):
                if j == min(4, FC - 1) and hook is not None:
                    hook()
                conv_paced(j, tH)
                jj = j % 2
                bg, fg = banks.get()
                bu, fu = banks.get()
                slot, ltok, n = use_unit(('g', l, j))
                t = None
                for kc in range(KC):
                    t = mm(psf[:, bg, 0:N], ring[:, slot, kc * 128:(kc + 1) * 128], xT[:, kc, 0:N], kc == 0, kc == KC - 1,
                           deps=([ltok] + fg + (list(xT_ready) if j == 0 else [])) if kc == 0 else (), inc=(kc == KC - 1))
                st.rel[n] = t
                tokG = t
                slot, ltok, n = use_unit(('u', l, j))
                for kc in range(KC):
                    t = mm(psf[:, bu, 0:N], ring[:, slot, kc * 128:(kc + 1) * 128], xT[:, kc, 0:N], kc == 0, kc == KC - 1,
                           deps=([ltok] + fu) if kc == 0 else (), inc=(kc == KC - 1))
                st.rel[n] = t
                tokU = t
                tS = S.op('act', f_act(sg[:, jj, 0:N], psf[:, bg, 0:N], AF.Silu), deps=[tokG, st.sg_free[jj]])
                tH = S.op('dve', f_stt(hT[:, j, 0:N], psf[:, bu, 0:N], 0.5, sg[:, jj, 0:N], ALU.mult, ALU.mult), deps=[tokU, tS])
                st.sg_free[jj] = tH
                banks.release(bg, [tS])
                banks.release(bu, [tH])
            return tH

        def proj_down(keyf, nkg, nktot, lhs, ablks, lhs_ready, after=None, want_xn=False, acc_ready=None, last_hook=None):
            nb = len(ablks)
            outs = []
            items = []
            xr = [None]

            def evac(t, c, b, tok):
                a = ablks[t]
                te = S.op('dve', f_stt(acc[:, a, c * 512:(c + 1) * 512], acc[:, a, c * 512:(c + 1) * 512], ALPHA,
                                       psf[:, b, 0:512], ALU.mult, ALU.add),
                          deps=[tok] + ([acc_ready[t]] if (acc_ready is not None and c == 0) else []))
                banks.release(b, [te])
                ts_ = S.op('dve', (lambda e, t=t, c=c, a=a: e.bn_stats(stt[:, t, c * 6:(c + 1) * 6], acc[:, a, c * 512:(c + 1) * 512])), deps=[te])
                return ts_

            first = True
            for c in range(3):
                bs = [banks.get() for _ in range(nb)]
                k = 0
                tok = None
                for kg in range(nkg):
                    slot, ltok, n = use_unit(keyf(c, kg))
                    nk = min(4, nktot - 4 * kg)
                    for kk in range(nk):
                        for t in range(nb):
                            deps = []
                            if kk == 0 and t == 0:
                                deps.append(ltok)
                            if k == 0:
                                deps += bs[t][1]
                                if first:
                                    deps += list(lhs_ready)
                                    first = False
                            last = (k == nktot - 1)
                            tok = mm(psf[:, bs[t][0], 0:512], lhs[:, k, t * 128:(t + 1) * 128],
                                     ring[:, slot, kk * 512:(kk + 1) * 512], k == 0, last, deps,
                                     inc=(t == nb - 1 and (last or kk == nk - 1)))
                        k += 1
                    st.rel[n] = tok
                for t in range(nb):
                    evac(t, c, bs[t][0], tok)
            c = 3
            for t in range(nb):
                b, fb = banks.get()
                k = 0
                tok = None
                for kg in range(nkg):
                    slot, ltok, n = use_unit(keyf(c, kg))
                    nk = min(4, nktot - 4 * kg)
                    for kk in range(nk):
                        deps = []
                        if kk == 0:
                            deps.append(ltok)
                        if k == 0:
                            deps += fb
                        last = (k == nktot - 1)
                        tok = mm(psf[:, b, 0:512], lhs[:, k, t * 128:(t + 1) * 128], ring[:, slot, kk * 512:(kk + 1) * 512],
                                 k == 0, last, deps, inc=(last or kk == nk - 1))
                        k += 1
                    st.rel[n] = tok
                ts_ = evac(t, c, b, tok)
                if last_hook is not None and t == nb - 1:
                    last_hook()
                tb, item = ln_block(t, ablks[t], [ts_], after, want_xn)
                outs.append(tb)
                items.append(item)
                if want_xn and t >= 2:
                    xr[0] = transpose_blk(t - 2, items[t - 2])
            if want_xn:
                for t in range(max(0, nb - 2), nb):
                    xr[0] = transpose_blk(t, items[t])
                outs = []
                tlast_xn = items[-1][1]
                for t in range(nb):
                    a = ablks[t]
                    tb = S.op('pool', f_tt(acc[:, a, :], acc[:, a, :], lnb[:, :], ALU.add), deps=[items[t][1], tlast_xn, st.ln_tok])
                    if after is not None:
                        after(t, a, tb)
                    st.ln_free = tb
                    st.accf[a] = tb
                    outs.append(tb)
                return outs, [xr[0]]
            return outs, None

        def kvu(ablks, runs, N, xT_ready, halo):
            nb = len(ablks)
            for g in range(2):
                b, fb = banks.get()
                slot, ltok, n = use_unit(('k', g))
                t = None
                for kc in range(KC):
                    t = mm(psf[:, b, 0:N], ring[:, slot, kc * 128:(kc + 1) * 128], xT[:, kc, 0:N], kc == 0, kc == KC - 1,
                           deps=([ltok] + fb + (list(xT_ready) if g == 0 else [])) if kc == 0 else (), inc=(kc == KC - 1))
                st.rel[n] = t
                te = None
                for (c0, ncol, db) in runs:
                    te = S.op('act', (lambda e, g=g, b=b, c0=c0, ncol=ncol, db=db: e.copy(out=KT[:, g, db * 128:db * 128 + ncol], in_=psf[:, b, c0:c0 + ncol])), deps=[t])
                banks.release(b, [te])
            bs = [banks.get() for _ in range(nb)]
            tok = None
            for i in range(2):
                slot, ltok, n = use_unit(('v', i))
                for kk in range(8):
                    kc = i * 8 + kk
                    for t in range(nb):
                        deps = []
                        if kk == 0 and t == 0:
                            deps.append(ltok)
                        if kc == 0:
                            deps += bs[t][1]
                        tok = mm(psf[:, bs[t][0], 0:256], xT[:, kc, t * 128:(t + 1) * 128], ring[:, slot, kk * 256:(kk + 1) * 256],
                                 kc == 0, kc == KC - 1, deps, inc=(t == nb - 1 and kk == 7))
                st.rel[n] = tok
            blkmap = []
            for (c0, ncol, db) in runs:
                for q in range(ncol // 128):
                    blkmap.append(db + q)
            for t in range(nb):
                te = S.op('dve', f_copy(V[:, blkmap[t], :], psf[:, bs[t][0], 0:256]), deps=[tok])
                banks.release(bs[t][0], [te])
            for c in range(4):
                jj = c % 2
                ba, fa = banks.get()
                bb, fbb = banks.get()
                slot, ltok, n = use_unit(('cc', c))
                t = None
                for kc in range(KC):
                    t = mm(psf[:, ba, 0:N], ring[:, slot, kc * 128:(kc + 1) * 128], xT[:, kc, 0:N], kc == 0, kc == KC - 1,
                           deps=([ltok] + fa) if kc == 0 else (), inc=(kc == KC - 1))
                st.rel[n] = t
                tA = t
                slot, ltok, n = use_unit(('ch', c))
                for kc in range(KC):
                    t = mm(psf[:, bb, 0:N], ring[:, slot, kc * 128:(kc + 1) * 128], xT[:, kc, 0:N], kc == 0, kc == KC - 1,
                           deps=([ltok] + fbb) if kc == 0 else (), inc=(kc == KC - 1))
                st.rel[n] = t
                tB = t
                tS = S.op('act', (lambda e, jj=jj, ba=ba: e.copy(out=sg[:, jj, 0:N], in_=psf[:, ba, 0:N])), deps=[tA, st.sg_free[jj]])
                if halo:
                    t1 = S.op('dve', f_stt(U[:, c, 0:1], psf[:, bb, 127:128], edg[:, 0:1], sg[:, jj, 127:128], ALU.mult, ALU.mult), deps=[tB, tS])
                    t1 = S.op('dve', f_stt(U[:, c, SEQT + 1:SEQT + 2], psf[:, bb, 128:129], edg[:, 1:2], sg[:, jj, 128:129], ALU.mult, ALU.mult), deps=[tB, tS])
                else:
                    (c0, ncol, db) = runs[0]
                    u0 = 1 + (db - 1) * 128
                    t1 = S.op('dve', f_tt(U[:, c, u0:u0 + N], psf[:, bb, 0:N], sg[:, jj, 0:N], ALU.mult), deps=[tB, tS])
                st.sg_free[jj] = t1
                banks.release(ba, [tS])
                banks.release(bb, [t1])

        def ffn_phase(l, ablks, N, li, xT_ready, after=None, want_xn=False, acc_ready=None, last_hook=None):
            tH = ffn_gateup(l, N, xT_ready, lambda: load_ln(li))
            return proj_down(lambda c, kg: ('dn', l, c, kg), NKG, FC, hT, ablks, [tH], after, want_xn, acc_ready, last_hook)

        def prefetch(nxt):
            kind = nxt[0]
            if kind == 'ffn1':
                s_, g_ = nxt[1], nxt[2]
                st.pref = [xn_from_dram(xs, s_ * SEQT + g_ * 512 + q * 128, []) for q in range(min(4, NXN))]
            elif kind == 'mix':
                g_ = nxt[2]
                st.pref = [xn_from_dram(x1scr, g_ * 512 + q * 128, [t for t in st.st_tok.values()]) for q in range(min(4, NXN))]

        def early_input(nxt):
            if nxt[0] == 'ffn1':
                src, row0, deps = xs, nxt[1] * SEQT + nxt[2] * 512, []
            else:
                src, row0, deps = x1scr, nxt[2] * 512, [t for t in st.st_tok.values()]
            items = st.pref if st.pref is not None else []
            st.pref = None
            st.pre_xr = make_xT_items(items, src, row0, deps, 4)

        def ffn1_phase(src, r0, ablks, runs, halo, x1_r0, ld=None, nxt=None):
            nb = len(ablks)
            N = nb * 128
            barrier()
            if halo:
                conv_slice(True)
                ld = load_acc(src, r0, ablks[0], nb)
                xr = make_xT(ablks, ld)
            else:
                if st.pre_xr is not None:
                    xr = st.pre_xr
                    st.pre_xr = None
                else:
                    items = st.pref if st.pref is not None else []
                    st.pref = None
                    xr = make_xT_items(items, src, r0, [], nb, ld)
                if ld is None:
                    ld = load_acc(src, r0, ablks[0], nb)
                conv_slice()
            after = None
            if not halo:
                after = lambda i, a, tb: store_blk(x1scr, x1_r0 + i * 128, a, tb)
            outs, xr = ffn_phase(1, ablks, N, 0, xr, after, True, ld)
            if nxt is not None:
                prefetch(nxt)
            kvu(ablks, runs, N, xr, halo)

        def mem_prep(s, ld=None):
            barrier(True)
            if ld is None:
                ld = load_acc(mem, s * NMEM, 0, 2)
            xr = make_xT([0, 1], ld)
            for h in range(4):
                b, fb = banks.get()
                slot, ltok, n = use_unit(('mk', h))
                t = None
                for kc in range(KC):
                    t = mm(psf[:, b, 0:NMEM], ring[:, slot, kc * 128:(kc + 1) * 128], xT[:, kc, 0:NMEM], kc == 0, kc == KC - 1,
                           deps=([ltok] + fb + (xr if h == 0 else [])) if kc == 0 else (), inc=(kc == KC - 1))
                st.rel[n] = t
                te = S.op('act', (lambda e, h=h, b=b: e.copy(out=mKT[:, h, :], in_=psf[:, b, 0:NMEM])), deps=[t])
                banks.release(b, [te])
            bs = [banks.get() for _ in range(2)]
            tok = None
            for i in range(4):
                slot, ltok, n = use_unit(('mv', i))
                for kk in range(4):
                    kc = i * 4 + kk
                    for t in range(2):
                        deps = []
                        if kk == 0 and t == 0:
                            deps.append(ltok)
                        if kc == 0:
                            deps += bs[t][1]
                        tok = mm(psf[:, bs[t][0], 0:512], xT[:, kc, t * 128:(t + 1) * 128], ring[:, slot, kk * 512:(kk + 1) * 512],
                                 kc == 0, kc == KC - 1, deps, inc=(t == 1 and kk == 3))
                st.rel[n] = tok
            for t in range(2):
                te = S.op('dve', f_copy(mV[:, t, :], psf[:, bs[t][0], 0:512]), deps=[tok])
                banks.release(bs[t][0], [te])

        def mix_phase(s, grp, halo_seq, nxt=None):
            ablks = [0, 1, 2, 3]
            t0 = grp * 512
            barrier()
            if st.pre_xr is not None:
                xr = st.pre_xr
                st.pre_xr = None
            else:
                items = st.pref if st.pref is not None else []
                st.pref = None
                xr = make_xT_items(items, x1scr, t0, [t for t in st.st_tok.values()], 4)
            ld = load_acc(x1scr, t0, 0, 4)
            conv_slice(True)
            tcu = None
            for c in range(4):
                tcu = S.op('dve', f_ts(cu[:, c, :], U[:, c, 1 + t0:1 + t0 + 512], cw[:, 3 * c + 1:3 * c + 2], None, ALU.mult))
                tcu = S.op('dve', f_stt(cu[:, c, :], U[:, c, t0:t0 + 512], cw[:, 3 * c:3 * c + 1], cu[:, c, :], ALU.mult, ALU.add), deps=[tcu])
                tcu = S.op('dve', f_stt(cu[:, c, :], U[:, c, 2 + t0:2 + t0 + 512], cw[:, 3 * c + 2:3 * c + 3], cu[:, c, :], ALU.mult, ALU.add), deps=[tcu])
            first = True
            tq = None
            for kind, cnt in (('q', 8), ('qx', 4), ('cb', 4)):
                for h in range(cnt):
                    b, fb = banks.get()
                    slot, ltok, n = use_unit((kind, h))
                    t = None
                    for kc in range(KC):
                        t = mm(psf[:, b, :], ring[:, slot, kc * 128:(kc + 1) * 128], xT[:, kc, :], kc == 0, kc == KC - 1,
                               deps=([ltok] + fb + (xr if first else [])) if kc == 0 else (), inc=(kc == KC - 1))
                    first = False
                    st.rel[n] = t
                    if kind == 'q':
                        te = S.op('act', (lambda e, h=h, b=b: e.copy(out=qT[:, h, :], in_=psf[:, b, :])), deps=[t])
                        tq = te
                    elif kind == 'qx':
                        te = S.op('act', (lambda e, h=h, b=b: e.copy(out=qxT[:, h, :], in_=psf[:, b, :])), deps=[t])
                        tqx = te
                    else:
                        te = S.op('dve', f_tt(ycat[:, 8 + h, :], psf[:, b, :], cu[:, h, :], ALU.mult), deps=[t, tcu])
                    banks.release(b, [te])
            pt_free = [None, None, None]
            stmp_free = [None, None]
            sgb = sg[:, :, :].rearrange("p a b -> p (a b)").bitcast(BF16).rearrange("p (a b) -> p a b", b=512)

            def PTs(un, jj):
                if un < 2:
                    return PT[:, un * 3 + jj, :]
                return sgb[:, jj, :]
            units = [('w', b, g) for b in range(4) for g in range(2)] + [('m', h, 0) for h in range(4)]
            sc = {'si': 0}
            tl = {'t': None}

            def score_stage(ui, unit):
                un = ui % 3
                extra = list(st.sg_free) if un == 2 else []
                kind = unit[0]
                pts = []
                if kind == 'w':
                    b, g = unit[1], unit[2]
                    bidx = 1 + 4 * grp + b
                    kbs = [(jj, bidx - 1 + jj) for jj in range(3)]
                    kbs = [(jj, kb) for jj, kb in kbs if halo_seq or (1 <= kb <= NBS)]
                    for jj, kb in kbs:
                        bsx, fs = banks.get()
                        ps3 = psf[:, bsx, :].rearrange("p (h t) -> p h t", h=4)
                        mm(ps3, KT[:, g, kb * 128:(kb + 1) * 128], qT[:, 4 * g:4 * g + 4, b * 128:(b + 1) * 128],
                           True, False, deps=fs + [tq], inc=False)
                        mm(ps3, ident[:, :], bhl[:, 0, jj, 4 * g:4 * g + 4, :], False, False, inc=False)
                        tqk = mm(ps3, ident[:, :], bhl[:, 1, jj, 4 * g:4 * g + 4, :], False, True, inc=True)
                        eb = None
                        if kb == 0:
                            eb = edg[:, 2:3]
                        elif kb == NBS + 1:
                            eb = edg[:, 3:4]
                        tp = S.op('act', f_act(PTs(un, jj), psf[:, bsx, :], AF.Exp, bias=eb, scale=SCALE),
                                  deps=[tqk, pt_free[un]] + extra)
                        banks.release(bsx, [tp])
                        pts.append((jj, kb, tp))
                else:
                    h = unit[1]
                    for j in range(2):
                        bsx, fs = banks.get()
                        tqk = mm(psf[:, bsx, :], mKT[:, h, j * 128:(j + 1) * 128], qxT[:, h, :], True, True, deps=fs + [tqx], inc=True)
                        tp = S.op('act', f_act(PTs(un, j), psf[:, bsx, :], AF.Exp, scale=SCALE), deps=[tqk, pt_free[un]] + extra)
                        banks.release(bsx, [tp])
                        pts.append((j, j, tp))
                return pts

            def pv_stage(ui, unit, pts):
                un = ui % 3
                u2 = ui % 2
                kind = unit[0]
                bo, fo = banks.get()
                bd, fd = banks.get()
                nk = len(pts)
                for i, (jj, kb, tp) in enumerate(pts):
                    if kind == 'w':
                        lhs = V[:, kb, unit[2] * 128:(unit[2] + 1) * 128]
                    else:
                        lhs = mV[:, jj, unit[1] * 128:(unit[1] + 1) * 128]
                    mm(psf[:, bo, :], lhs, PTs(un, jj), i == 0, i == nk - 1, deps=[tp] + (fo if i == 0 else []), inc=False)
                to = None
                for i, (jj, kb, tp) in enumerate(pts):
                    to = mm(psf[:, bd, :], ones[:, :], PTs(un, jj), i == 0, i == nk - 1,
                            deps=(fd if i == 0 else []), inc=(i == nk - 1))
                pt_free[un] = to
                if kind == 'w':
                    b, g = unit[1], unit[2]
                    d3 = dn[:, u2, :].rearrange("p (h t) -> p h t", h=4)
                    t1 = S.op('dve', f_tt(d3, psf[:, bd, :].rearrange("p (h t) -> p h t", h=4),
                                          esk[:, 4 * g:4 * g + 4].unsqueeze(2).to_broadcast([128, 4, 128]), ALU.add), deps=[to])
                    t2 = S.op('dve', (lambda e, un=u2: e.reciprocal(dn[:, un, :], dn[:, un, :])), deps=[t1])
                    t3 = S.op('dve', f_tt(ycat[:, 4 * g:4 * g + 4, b * 128:(b + 1) * 128],
                                          psf[:, bo, :].rearrange("p (h t) -> p h t", h=4), d3, ALU.mult), deps=[t2])
                    banks.release(bd, [t1])
                else:
                    h = unit[1]
                    t2 = S.op('dve', (lambda e, un=u2, bd=bd: e.reciprocal(dn[:, un, :], psf[:, bd, :])), deps=[to])
                    t3 = S.op('dve', f_tt(ycat[:, 12 + h, :], psf[:, bo, :], dn[:, u2, :], ALU.mult), deps=[t2])
                    banks.release(bd, [t2])
                banks.release(bo, [t3])
                tl['t'] = t3

            pend = []
            for ui, unit in enumerate(units):
                pts = score_stage(ui, unit)
                pend.append((ui, unit, pts))
                if len(pend) > 2:
                    pv_stage(*pend.pop(0))
            while pend:
                pv_stage(*pend.pop(0))
            tlast = tl['t']
            st.sg_free = [pt_free[2], pt_free[2]]
            load_ln(1)
            gi = 0
            tM = None
            for m in range(16):
                slotB, ltB, nB = use_unit(('wb', m))
                mi = m % 2
                for br, (k0, k1) in enumerate(((0, 8), (8, 12), (12, 16))):
                    bz, fz = banks.get()
                    tz = None
                    for kc in range(k0, k1):
                        tz = mm(psf[:, bz, :], ring[:, slotB, kc * 128:(kc + 1) * 128], ycat[:, kc, :], kc == k0, kc == k1 - 1,
                                deps=(fz + [ltB, tlast]) if kc == k0 else (), inc=(kc == k1 - 1))
                    if br == 2:
                        st.rel[nB] = tz
                    slotG, ltG, nG = use_unit(('wg', m, br))
                    bgx, fg = banks.get()
                    tg = None
                    for kc in range(KC):
                        tg = mm(psf[:, bgx, :], ring[:, slotG, kc * 128:(kc + 1) * 128], xT[:, kc, :], kc == 0, kc == KC - 1,
                                deps=(fg + [ltG]) if kc == 0 else (), inc=(kc == KC - 1))
                    st.rel[nG] = tg
                    sx = gi % 2
                    gi += 1
                    tS = S.op('act', f_act(sgt[:, sx, :], psf[:, bgx, :], AF.Sigmoid, bias=bgt[:, br * 16 + m:br * 16 + m + 1]),
                              deps=[tg, tM])
                    if br == 0:
                        tM = S.op('dve', f_tt(tac[:, mi, :], psf[:, bz, :], sgt[:, sx, :], ALU.mult), deps=[tz, tS])
                        tzr = tM
                    elif br == 1:
                        tzr = S.op('dve', f_tt(tm2[:, mi, :], psf[:, bz, :], sgt[:, sx, :], ALU.mult), deps=[tz, tS])
                        tM = S.op('dve', f_tt(tac[:, mi, :], tac[:, mi, :], tm2[:, mi, :], ALU.add), deps=[tzr])
                    else:
                        tzr = S.op('dve', f_tt(tm2[:, mi, :], psf[:, bz, :], sgt[:, sx, :], ALU.mult), deps=[tz, tS])
                        tM = S.op('dve', f_tt(merged[:, m, :], tac[:, mi, :], tm2[:, mi, :], ALU.add), deps=[tzr])
                    banks.release(bz, [tzr])
                    banks.release(bgx, [tS])
            outs, xr = proj_down(lambda c, kg: ('wo', c, kg), 4, 16, merged, ablks, [tM], None, True, ld)
            if nxt is not None:
                prefetch(nxt)
            after = lambda i, a, tb: store_blk(ys, s * SEQT + t0 + i * 128, a, tb)
            outs, _ = ffn_phase(2, ablks, 512, 2, xr, after, False, outs,
                                (lambda: early_input(nxt)) if nxt is not None else None)

        def setup():
            c = []
            c.append(S.dma('sp', 'cs', f_dma(tab[0:32, :], table[:, :])))
            c.append(S.dma('sp', 'cs', f_dma(ohs[:, :], oh[:, :])))
            c.append(S.dma('sp', 'cs', f_dma(idf[:, :], identin[:, :])))
            c.append(S.dma('sp', 'cs', f_dma(bgt[:, :], bgate[:, :])))
            c.append(S.dma('sp', 'cs', f_dma(cw[:, :], convw[:, :])))
            c.append(S.dma('sp', 'cs', f_dma(edg[:, :], edge[:, :])))
            c.append(S.dma('sp', 'cs', f_dma(esk[:, :], bass.AP(sink.tensor, 0, [[0, 128], [1, 8]]))))
            cs = c[-1]
            t = S.op('dve', f_memset(tab[32:33, :], 1.0))
            t = S.op('dve', f_memset(ones[:, :], 1.0))
            t = S.op('dve', f_memset(U[:, :, 0:1], 0.0))
            t = S.op('dve', f_memset(U[:, :, SEQT + 1:SEQT + 2], 0.0))
            if not cfg.HALO:
                pass
            ti = S.op('act', (lambda e: e.copy(out=ident[:, :], in_=idf[:, :])), deps=[cs])
            te = S.op('act', f_act(esk[:, :], esk[:, :], AF.Exp), deps=[cs])
            tg = None
            for blk in range(3):
                b, fb = banks.get()
                tmm = mm(psf[0:8, b, 0:255], tab[:, :], ohs[:, blk * 255:(blk + 1) * 255], True, True, deps=[cs, t] + fb, inc=True)
                tg = S.op('dve', f_copy(gv[:, blk * 255:(blk + 1) * 255], psf[0:8, b, 0:255]), deps=[tmm])
                banks.release(b, [tg])
            d1 = S.dma('sp', 'cs', f_dma(bass.AP(gsc.tensor, 0, [[255, 8], [8 * 255, 3], [1, 255]]),
                                         gv[:, :].rearrange("p (b s) -> p b s", s=255)), deps=[tg])
            d2 = S.dma('sp', 'cs', f_dma(bass.AP(zsc.tensor, 0, [[128 * 255, 24], [255, 128], [1, 255]]),
                                         bass.AP(gsc.tensor, 0, [[255, 24], [0, 128], [1, 255]])), deps=[d1])
            btmp = work[:, 8192:8192 + 6144].bitcast(F32)
            btmp2 = work[:, 14336:14336 + 6144].bitcast(F32)
            d3 = S.dma('sp', 'cs', f_dma(btmp.rearrange("p (i t) -> p i t", t=128),
                                         bass.AP(zsc.tensor, 0, [[254, 128], [128 * 255, 24], [1, 128]])), deps=[d2])
            bhi = bhl[:, 0, :, :, :].rearrange("p b h t -> p (b h t)")
            blo = bhl[:, 1, :, :, :].rearrange("p b h t -> p (b h t)")
            tb_ = S.op('dve', f_ts(btmp, btmp, 1.0 / SCALE, None, ALU.mult), deps=[d3])
            tb_ = S.op('dve', f_copy(bhi, btmp), deps=[tb_])
            tb_ = S.op('dve', f_tt(btmp2, btmp, bhi, ALU.subtract), deps=[tb_])
            tb_ = S.op('dve', f_copy(blo, btmp2), deps=[tb_])
            for e in ('pe', 'act', 'dve', 'pool'):
                S.wait(e, [d3, ti, te, t, tb_])

        setup()
        ld_mem0 = load_acc(mem, 0, 0, 2)
        c_split = (offs[('q', 0)][0] + CW - 1) // CW
        c_mem = (offs[('g', 1, 0)][0] + CW - 1) // CW
        conversions(0, c_mem)
        st.ldx = None
        st.conv_rest = (c_split, NCH)
        st.conv_per = (NCH - c_split + NG - 1) // NG
        phases = []
        for s in range(NSEQ):
            halo_seq = (s == 0 and cfg.HALO)
            phases.append(('mem', s))
            for g in range(NG):
                phases.append(('ffn1', s, g))
            if halo_seq:
                phases.append(('halo',))
            for g in range(NG):
                phases.append(('mix', s, g, halo_seq))
        for pi, ph in enumerate(phases):
            nxt = phases[pi + 1] if pi + 1 < len(phases) else None
            if nxt is not None and nxt[0] not in ('ffn1', 'mix'):
                nxt = None
            kind = ph[0]
            if kind == 'mem':
                s = ph[1]
                if s == 1 and cfg.HALO:
                    S.op('dve', f_memset(U[:, :, 0:1], 0.0))
                    S.op('dve', f_memset(U[:, :, SEQT + 1:SEQT + 2], 0.0))
                mem_prep(s, ld_mem0 if s == 0 else None)
                if s == 0:
                    st.pref = [xn_from_dram(xs, q * 128, []) for q in range(min(4, NXN))]
                    st.ldx = load_acc(xs, 0, 0, 4)
                    conversions(c_mem, c_split)
            elif kind == 'ffn1':
                s, g = ph[1], ph[2]
                ldx = st.ldx
                st.ldx = None
                ffn1_phase(xs, s * SEQT + g * 512, [0, 1, 2, 3], [(0, 512, 1 + 4 * g)], False, g * 512, ldx, nxt)
            elif kind == 'halo':
                ffn1_phase(xh, 0, [2, 3], [(0, 128, 0), (128, 128, NBS + 1)], True, 0, None, nxt)
            else:
                mix_phase(ph[1], ph[2], ph[3], nxt)
        S.wait('pool', list(st.st_tok.values()))

        semnames = set()
        for e in ENG:
            for o in S.ops[e]:
                if o[0] == 'w':
                    semnames.add(o[1])
                elif o[0] == 'i':
                    semnames.add(o[2])
        sems = {k: es.enter_context(nc.semaphore("s_" + k)) for k in sorted(semnames)}
        block = es.enter_context(nc.Block())

        def emit(eng_obj, ops):
            for o in ops:
                if o[0] == 'w':
                    eng_obj.wait_ge(sems[o[1]], o[2])
                elif o[0] == 'i':
                    o[1](eng_obj).then_inc(sems[o[2]], o[3])
                else:
                    o[1](eng_obj)

        @block.tensor
        def _(e):
            emit(e, S.ops['pe'])

        @block.scalar
        def _(e):
            emit(e, S.ops['act'])

        @block.vector
        def _(e):
            emit(e, S.ops['dve'])

        @block.gpsimd
        def _(e):
            emit(e, S.ops['pool'])

        @block.sync
        def _(e):
            emit(e, S.ops['sp'])
    return nc


def host_consts(rel_bias_table, ln1_g, ln1_b, ln2_g, ln2_b, ln3_g, ln3_b, b_gate, conv_w, attn_sink):
    lnp = np.stack([ln1_g[0], ln1_b[0], ln2_g[0], ln2_b[0], ln3_g[0], ln3_b[0]]).astype(np.float32)
    bg = np.ascontiguousarray(b_gate[0].reshape(3, 16, 128).transpose(2, 0, 1)).reshape(128, 48)
    cwv = np.ascontiguousarray(conv_w[0].reshape(3, 4, 128).transpose(2, 1, 0)).reshape(128, 12)
    return {
        "lnp": np.ascontiguousarray(lnp), "bgate": np.ascontiguousarray(bg), "convw": np.ascontiguousarray(cwv),
        "sink": np.ascontiguousarray(attn_sink.reshape(1, 8)), "table": np.ascontiguousarray(rel_bias_table),
        "oh": bias_onehot(), "identin": np.eye(128, dtype=np.float32),
    }


def kernel(x_prompt, x_sample, mem_prompt, mem_sample, rel_bias_table, ln1_g, ln1_b, ffn1_w_gate_up,
           ffn1_w_down, w_in, conv_w, w_mem_kv, attn_sink, w_gate, b_gate, w_branch, w_o, ln2_g, ln2_b,
           ffn2_w_gate_up, ffn2_w_down, ln3_g, ln3_b):
    f = lambda a: np.asarray(a, dtype=np.float32)
    x_prompt, x_sample, mem_prompt, mem_sample = f(x_prompt), f(x_sample), f(mem_prompt), f(mem_sample)
    cfg = Cfg()
    wflat = pack_weights(cfg, f(ffn1_w_gate_up)[0], f(ffn1_w_down)[0], f(w_in)[0], f(w_mem_kv)[0], f(w_gate)[0],
                         f(w_branch)[0], f(w_o)[0], f(ffn2_w_gate_up)[0], f(ffn2_w_down)[0])
    consts = host_consts(f(rel_bias_table), f(ln1_g), f(ln1_b), f(ln2_g), f(ln2_b), f(ln3_g), f(ln3_b),
                         f(b_gate), f(conv_w), f(attn_sink))
    nc = build_program(cfg)
    in_maps = []
    CH = 2048
    for i in range(8):
        sb_, ch = i // 4, i % 4
        xs = np.concatenate([x_sample[sb_, ch * CH:(ch + 1) * CH], x_prompt[2 * i], x_prompt[2 * i + 1]], axis=0)
        xh = np.zeros((256, D), np.float32)
        fl, fr = 0.0, 0.0
        if ch > 0:
            xh[0:128] = x_sample[sb_, ch * CH - 128:ch * CH]
            fl = 1.0
        if ch < 3:
            xh[128:256] = x_sample[sb_, (ch + 1) * CH:(ch + 1) * CH + 128]
            fr = 1.0
        mem = np.concatenate([mem_sample[sb_], mem_prompt[2 * i], mem_prompt[2 * i + 1]], axis=0)
        edge = np.tile(np.array([[fl, fr, 0.0 if fl else NEG, 0.0 if fr else NEG]], np.float32), (128, 1))
        m = {"xs": np.ascontiguousarray(xs), "xh": xh, "mem": np.ascontiguousarray(mem), "wflat": wflat, "edge": edge}
        m.update(consts)
        in_maps.append(m)
    res = run_bass_kernel_spmd(nc, in_maps, core_ids=list(range(8)))
    y_prompt = np.zeros((16, 2048, D), np.float32)
    y_sample = np.zeros((2, 8192, D), np.float32)
    for i in range(8):
        ysd = np.asarray(res.results[i]["ys"], dtype=np.float32)
        sb_, ch = i // 4, i % 4
        y_sample[sb_, ch * CH:(ch + 1) * CH] = ysd[0:CH]
        y_prompt[2 * i] = ysd[CH:2 * CH]
        y_prompt[2 * i + 1] = ysd[2 * CH:3 * CH]
    return (y_prompt, y_sample)
```

```python
from contextlib import ExitStack
import numpy as np
import concourse.bass as bass
import concourse.mybir as mybir
from concourse.bass_utils import run_bass_kernel_spmd

F32 = mybir.dt.float32
BF16 = mybir.dt.bfloat16
AF = mybir.ActivationFunctionType
ALU = mybir.AluOpType

D = 2048
KC = 16
HD = 128
NMEM = 256
ALPHA = 2.0 ** 0.25
SCALE = 128.0 ** -0.5
EPS = 1e-5
NEG = -30000.0
NSLOT = 6
NXN = 3
UL = 2048
CW = 1024
NCV = 8
ENG = ('pe', 'act', 'dve', 'pool', 'sp')


class Cfg:
    def __init__(self, FF=5504, NBS=16, NSEQ=3, HALO=True):
        self.FF = FF
        self.FC = FF // 128
        self.NBS = NBS
        self.NG = NBS // 4
        self.NSEQ = NSEQ
        self.HALO = HALO
        self.NKG = (self.FC + 3) // 4


def _stat(W, col0, ncol=128):
    kc = W.shape[0] // 128
    return np.ascontiguousarray(W[:, col0:col0 + ncol].reshape(kc, 128, ncol).transpose(1, 0, 2)).reshape(128, kc * ncol)


def _mov(W, k0, nk, col0, ncols):
    return np.ascontiguousarray(W[k0 * 128:(k0 + nk) * 128, col0:col0 + ncols].reshape(nk, 128, ncols).transpose(1, 0, 2)).reshape(128, nk * ncols)


def unit_list(cfg):
    FC, NKG = cfg.FC, cfg.NKG
    L = []
    for h in range(4):
        L.append((('mk', h), 2048))
    for i in range(4):
        L.append((('mv', i), 2048))
    for l in (1, 2):
        if l == 2:
            for g in range(2):
                L.append((('k', g), 2048))
            for i in range(2):
                L.append((('v', i), 2048))
            for c in range(4):
                L.append((('cc', c), 2048))
                L.append((('ch', c), 2048))
            for h in range(8):
                L.append((('q', h), 2048))
            for h in range(4):
                L.append((('qx', h), 2048))
            for c in range(4):
                L.append((('cb', c), 2048))
            for m in range(16):
                L.append((('wb', m), 2048))
                for br in range(3):
                    L.append((('wg', m, br), 2048))
            for c in range(4):
                for kg in range(4):
                    L.append((('wo', c, kg), 2048))
        for j in range(FC):
            L.append((('g', l, j), 2048))
            L.append((('u', l, j), 2048))
        for c in range(4):
            for kg in range(NKG):
                nk = min(4, FC - 4 * kg)
                L.append((('dn', l, c, kg), nk * 512))
    return L


def unit_offsets(cfg):
    offs = {}
    o = 0
    for key, ln in unit_list(cfg):
        offs[key] = (o, ln)
        o += ln
    tot = ((o + CW - 1) // CW) * CW
    return offs, tot


def pack_weights(cfg, ffn1_w_gate_up, ffn1_w_down, w_in, w_mem_kv, w_gate, w_branch, w_o,
                 ffn2_w_gate_up, ffn2_w_down):
    offs, tot = unit_offsets(cfg)
    FF, FC = cfg.FF, cfg.FC
    out = np.zeros((128, tot), np.float32)
    wgu = {1: ffn1_w_gate_up, 2: ffn2_w_gate_up}
    wdn = {1: ffn1_w_down, 2: ffn2_w_down}
    for key, (o, ln) in offs.items():
        t = key[0]
        if t == 'mk':
            a = _stat(w_mem_kv, key[1] * 128)
        elif t == 'mv':
            a = _mov(w_mem_kv, key[1] * 4, 4, 512, 512)
        elif t == 'k':
            a = _stat(w_in, 1024 + key[1] * 128)
        elif t == 'v':
            a = _mov(w_in, key[1] * 8, 8, 1280, 256)
        elif t == 'cc':
            a = _stat(w_in, 2048 + key[1] * 128)
        elif t == 'ch':
            a = _stat(w_in, 2560 + key[1] * 128)
        elif t == 'q':
            a = _stat(w_in, key[1] * 128)
        elif t == 'qx':
            a = _stat(w_in, 3072 + key[1] * 128)
        elif t == 'cb':
            a = _stat(w_in, 1536 + key[1] * 128)
        elif t == 'wb':
            a = _stat(w_branch, key[1] * 128)
        elif t == 'wg':
            a = _stat(w_gate, key[2] * D + key[1] * 128)
        elif t == 'wo':
            a = _mov(w_o, key[2] * 4, 4, key[1] * 512, 512)
        elif t == 'g':
            a = _stat(wgu[key[1]], key[2] * 128)
        elif t == 'u':
            a = _stat(wgu[key[1]], FF + key[2] * 128)
        elif t == 'dn':
            nk = min(4, FC - 4 * key[3])
            a = _mov(wdn[key[1]], key[3] * 4, nk, key[2] * 512, 512)
        else:
            raise KeyError(key)
        assert a.shape == (128, ln), (key, a.shape, ln)
        out[:, o:o + ln] = a
    return out


def bias_onehot():
    half = 16
    max_exact = 8
    oh = np.zeros((33, 3 * 255), np.float32)
    for blk in range(3):
        for m in range(255):
            d = m if m <= 127 else m - 255
            rel = -d + 128 * (blk - 1)
            n = abs(rel)
            if n > 128:
                oh[32, blk * 255 + m] = NEG
                continue
            large = max_exact + (np.log(np.maximum(n, 1) / max_exact) / np.log(128 / max_exact) * (half - max_exact)).astype(np.int32)
            large = min(int(large), half - 1)
            bucket = int(rel > 0) * half + (n if n < max_exact else large)
            oh[bucket, blk * 255 + m] = 1.0
    return oh


class Sched:
    def __init__(self):
        self.ops = {e: [] for e in ENG}
        self.cnt = {}
        self.waited = {e: {} for e in ENG}

    def wait(self, eng, deps):
        for d in deps:
            if d is None:
                continue
            k, v = d
            if self.waited[eng].get(k, 0) >= v:
                continue
            self.waited[eng][k] = v
            self.ops[eng].append(('w', k, v))

    def op(self, eng, fn, deps=(), inc=True):
        self.wait(eng, deps)
        if inc:
            c = self.cnt.get(eng, 0) + 1
            self.cnt[eng] = c
            self.ops[eng].append(('i', fn, eng, 1))
            return (eng, c)
        self.ops[eng].append(('n', fn))
        return None

    def dma(self, eng, sem, fn, deps=()):
        self.wait(eng, deps)
        c = self.cnt.get(sem, 0) + 16
        self.cnt[sem] = c
        self.ops[eng].append(('i', fn, sem, 16))
        return (sem, c)


class Banks:
    def __init__(self, n):
        self.n = n
        self.nxt = 0
        self.free = [[] for _ in range(n)]

    def get(self):
        b = self.nxt
        self.nxt = (b + 1) % self.n
        return b, list(self.free[b])

    def release(self, b, toks):
        self.free[b] = list(toks)


def f_mm(out, lhsT, rhs, start, stop):
    return lambda e: e.matmul(out, lhsT=lhsT, rhs=rhs, start=start, stop=stop)


def f_tr(out, in_, ident):
    return lambda e: e.transpose(out, in_, ident)


def f_act(out, in_, func, bias=None, scale=None):
    kw = {}
    if bias is not None:
        kw['bias'] = bias
    if scale is not None:
        kw['scale'] = scale
    return lambda e: e.activation(out=out, in_=in_, func=func, **kw)


def f_stt(out, in0, scalar, in1, op0, op1):
    return lambda e: e.scalar_tensor_tensor(out=out, in0=in0, scalar=scalar, in1=in1, op0=op0, op1=op1)


def f_ts(out, in0, s1, s2, op0, op1=None):
    if op1 is None:
        return lambda e: e.tensor_scalar(out=out, in0=in0, scalar1=s1, scalar2=None, op0=op0)
    return lambda e: e.tensor_scalar(out=out, in0=in0, scalar1=s1, scalar2=s2, op0=op0, op1=op1)


def f_tt(out, in0, in1, op):
    return lambda e: e.tensor_tensor(out=out, in0=in0, in1=in1, op=op)


def f_copy(out, in_):
    return lambda e: e.tensor_copy(out=out, in_=in_)


def f_dma(out, in_):
    return lambda e: e.dma_start(out=out, in_=in_)


def f_memset(ap, v):
    return lambda e: e.memset(ap, v)


def build_program(cfg):
    FC, NBS, NG, NSEQ, NKG = cfg.FC, cfg.NBS, cfg.NG, cfg.NSEQ, cfg.NKG
    offs, TOT = unit_offsets(cfg)
    NCH = TOT // CW
    SEQT = NBS * 128
    nc = bass.Bass("TRN2", target_bir_lowering=False)

    def din(name, shape, dt=F32):
        return nc.dram_tensor(name, shape, dt, kind="ExternalInput").ap()

    xs = din("xs", [NSEQ * SEQT, D])
    xh = din("xh", [256, D])
    mem = din("mem", [NSEQ * NMEM, D])
    wflat = din("wflat", [128, TOT])
    lnp = din("lnp", [6, D])
    bgate = din("bgate", [128, 48])
    convw = din("convw", [128, 12])
    sink = din("sink", [1, 8])
    table = din("table", [32, 8])
    oh = din("oh", [33, 765])
    edge = din("edge", [128, 4])
    identin = din("identin", [128, 128])
    ys = nc.dram_tensor("ys", [NSEQ * SEQT, D], F32, kind="ExternalOutput").ap()
    wscr = nc.dram_tensor("wscr", [128, TOT], BF16, kind="Internal").ap()
    x1scr = nc.dram_tensor("x1scr", [SEQT, D], F32, kind="Internal").ap()
    gsc = nc.dram_tensor("gsc", [24 * 255], F32, kind="Internal").ap()
    zsc = nc.dram_tensor("zsc", [24 * 128 * 255], F32, kind="Internal").ap()

    S = Sched()
    banks = Banks(7)
    es = ExitStack()

    def sb(name, shape, dt):
        return es.enter_context(nc.sbuf_tensor(name, shape, dt))

    with es:
        KT = sb("KT", [128, 2, (NBS + 2) * 128], BF16)
        V = sb("V", [128, NBS + 2, 256], BF16)
        U = sb("U", [128, 4, SEQT + 2], BF16)
        mKT = sb("mKT", [128, 4, NMEM], BF16)
        mV = sb("mV", [128, 2, 512], BF16)
        bhl = sb("bhl", [128, 2, 3, 8, 128], BF16)
        lng = sb("lng", [128, D], F32)
        lnb = sb("lnb", [128, D], F32)
        esk = sb("esk", [128, 8], F32)
        ident = sb("ident", [128, 128], BF16)
        ones = sb("ones", [128, 128], BF16)
        bgt = sb("bgt", [128, 48], F32)
        cw = sb("cw", [128, 12], F32)
        edg = sb("edg", [128, 4], F32)
        ring = sb("ring", [128, NSLOT, UL], BF16)
        acc = sb("acc", [128, 4, D], F32)
        xn = sb("xn", [128, NXN, D], BF16)
        xT = sb("xT", [128, KC, 512], BF16)
        work = sb("work", [128, 25600], BF16)
        sg = sb("sg", [128, 2, 512], F32)
        stt = sb("stt", [128, 4, 24], F32)
        mv = sb("mv", [128, 4, 2], F32)
        rs = sb("rs", [128, 4, 2], F32)
        psf = es.enter_context(nc.psum_tensor("psf", [128, 7, 512], F32))
        pst = es.enter_context(nc.psum_tensor("pst", [128, 1024], BF16))

        def wv(b0, b1, dt, inner=None):
            v = work[:, b0 // 2:b1 // 2]
            if dt == F32:
                v = v.bitcast(F32)
            if inner:
                v = v.rearrange("p (a b) -> p a b", b=inner)
            return v

        tab = work[0:33, 0:16].bitcast(F32)
        ohs = work[0:33, 16:16 + 1530].bitcast(F32)
        gv = work[0:8, 2048:2048 + 1530].bitcast(F32)
        idf = work[:, 4096:4096 + 256].bitcast(F32)
        hT = wv(0, FC * 1024, BF16, 512)
        qT = wv(0, 8192, BF16, 512)
        qxT = wv(8192, 12288, BF16, 512)
        cu = wv(12288, 20480, F32, 512)
        merged = wv(0, 16384, BF16, 512)
        PT = wv(20480, 26624, BF16, 512)
        stmp = wv(26624, 30720, F32, 512)
        dn = wv(30720, 34816, F32, 512)
        sgt = wv(20480, 24576, F32, 512)
        tac = wv(24576, 28672, F32, 512)
        tm2 = wv(28672, 32768, F32, 512)
        ycat = wv(34816, 51200, BF16, 512)

        class St:
            pass
        st = St()
        st.nunits = 0
        st.rel = {}
        st.xn_free = [None] * NXN
        st.pref = None
        st.pre_xr = None
        st.conv_todo = []
        st.xn_i = 0
        st.st_tok = {}
        st.accf = {}
        st.pst_free = None
        st.sg_free = [None, None]
        st.acc_free = []
        st.store_tok = None
        st.x1_store = None
        st.ln_tok = None
        st.cnt2 = 0

        def conv_token(col_end):
            ci = (col_end - 1) // CW
            return ('cv%d' % (ci % NCV), 16 * (ci // NCV + 1))

        def use_unit(key):
            off, L = offs[key]
            n = st.nunits
            st.nunits += 1
            slot = n % NSLOT
            deps = [('cv%d' % (ci % NCV), 16 * (ci // NCV + 1)) for ci in range(off // CW, (off + L - 1) // CW + 1)]
            if n >= NSLOT:
                deps.append(st.rel[n - NSLOT])
            tok = S.dma('sp', 'wl%d' % slot, f_dma(ring[:, slot, 0:L], wscr[:, off:off + L]), deps)
            return slot, tok, n

        def mm(out, lhsT, rhs, start, stop, deps=(), inc=False):
            return S.op('pe', f_mm(out, lhsT, rhs, start, stop), deps, inc)

        def barrier(full=False):
            if not full:
                return
            toks = [(e, S.cnt[e]) for e in ('pe', 'act', 'dve') if S.cnt.get(e, 0) > 0]
            for e in ('pe', 'act', 'dve'):
                S.wait(e, toks)

        def conversions(c0, c1, nfl=NCV):
            for ci in range(c0, c1):
                sem = 'cv%d' % (ci % NCV)
                deps = []
                cp = ci - nfl
                if cp >= 0:
                    deps.append(('cv%d' % (cp % NCV), 16 * (cp // NCV + 1)))
                S.dma('pool', sem, f_dma(wscr[:, ci * CW:(ci + 1) * CW], wflat[:, ci * CW:(ci + 1) * CW]), deps)

        def conv_slice(all_=False):
            if st.conv_rest is None:
                return
            c0, c1 = st.conv_rest
            n = (c1 - c0) if all_ else min(c1 - c0, st.conv_per)
            if all_:
                conversions(c0, c0 + n, 2)
            else:
                st.conv_todo = list(range(c0, c0 + n))
            st.conv_rest = (c0 + n, c1) if c0 + n < c1 else None

        def conv_paced(j, pace_tok):
            todo = st.conv_todo
            if not todo:
                return
            k = -(-len(todo) // max(1, FC - j))
            for _ in range(k):
                ci = todo.pop(0)
                sem = 'cv%d' % (ci % NCV)
                deps = [pace_tok]
                cp = ci - NCV
                if cp >= 0:
                    deps.append(('cv%d' % (cp % NCV), 16 * (cp // NCV + 1)))
                S.dma('pool', sem, f_dma(wscr[:, ci * CW:(ci + 1) * CW], wflat[:, ci * CW:(ci + 1) * CW]), deps)

        def load_acc(src, r0, a0, nb):
            toks = []
            for q in range(nb):
                a = a0 + q
                deps = [st.st_tok.get(a), st.accf.get(a)]
                toks.append(S.dma('pool', 'ld%d' % a, f_dma(acc[:, a, :], src[r0 + q * 128:r0 + (q + 1) * 128, :]), deps))
            return toks

        def load_ln(li):
            deps = [st.ln_free] if getattr(st, 'ln_free', None) else []
            t1 = S.dma('sp', 'lnl', f_dma(lng[:, :], bass.AP(lnp.tensor, 2 * li * D, [[0, 128], [1, D]])), deps)
            t2 = S.dma('sp', 'lnl', f_dma(lnb[:, :], bass.AP(lnp.tensor, (2 * li + 1) * D, [[0, 128], [1, D]])), deps)
            st.ln_tok = t2

        def xn_alloc():
            xi = st.xn_i % NXN
            st.xn_i += 1
            return xi

        def xn_from_acc(a, ready):
            xi = xn_alloc()
            tok = S.op('act', lambda e, a=a, xi=xi: e.copy(out=xn[:, xi, :], in_=acc[:, a, :]), deps=[ready, st.xn_free[xi]])
            st.accf[a] = tok
            return (xi, tok)

        def xn_from_dram(src, row0, deps):
            xi = xn_alloc()
            tok = S.dma('pool', 'xn%d' % xi, f_dma(xn[:, xi, :], src[row0:row0 + 128, :]), [st.xn_free[xi]] + list(deps))
            return (xi, tok)

        def transpose_blk(i, item):
            xi, tok = item
            tk = None
            t_ev = None
            for half in range(2):
                for k8 in range(8):
                    kc = half * 8 + k8
                    tk = S.op('pe', f_tr(pst[:, k8 * 128:(k8 + 1) * 128], xn[:, xi, kc * 128:(kc + 1) * 128], ident[:, :]),
                              deps=[tok, st.pst_free] if k8 == 0 else (), inc=(k8 == 7))
                t_ev = S.op('dve', f_copy(xT[:, half * 8:(half + 1) * 8, i * 128:(i + 1) * 128],
                                          pst[:, :].rearrange("p (k t) -> p k t", t=128)), deps=[tk])
                st.pst_free = t_ev
            st.xn_free[xi] = tk
            return t_ev

        def make_xT(ablks, ready):
            t_ev = None
            for i, a in enumerate(ablks):
                t_ev = transpose_blk(i, xn_from_acc(a, ready[i]))
            return [t_ev]

        def make_xT_items(items, src, row0, deps, nblk, fallback=None):
            items = list(items)
            t_ev = None
            for i in range(nblk):
                if i >= len(items):
                    if fallback is not None:
                        items.append(xn_from_acc(i, fallback[i]))
                    else:
                        items.append(xn_from_dram(src, row0 + i * 128, deps))
                t_ev = transpose_blk(i, items[i])
            return [t_ev]

        def ln_block(i, a, ready, after=None, want_xn=False):
            t = S.op('dve', (lambda e, i=i: e.bn_aggr(mv[:, i, :], stt[:, i, :])), deps=list(ready))
            t = S.op('dve', f_ts(rs[:, i, 0:1], mv[:, i, 1:2], EPS, None, ALU.add), deps=[t])
            t = S.op('act', f_act(rs[:, i, 0:1], rs[:, i, 0:1], AF.Sqrt), deps=[t])
            t = S.op('dve', (lambda e, i=i: e.reciprocal(rs[:, i, 0:1], rs[:, i, 0:1])), deps=[t])
            t = S.op('dve', f_stt(rs[:, i, 1:2], mv[:, i, 0:1], -1.0, rs[:, i, 0:1], ALU.mult, ALU.mult), deps=[t])
            tn = S.op('act', f_act(acc[:, a, :], acc[:, a, :], AF.Identity, bias=rs[:, i, 1:2], scale=rs[:, i, 0:1]), deps=[t])
            tg = S.op('dve', f_tt(acc[:, a, :], acc[:, a, :], lng[:, :], ALU.mult), deps=[tn, st.ln_tok])
            item = None
            if want_xn:
                xi = xn_alloc()
                txn = S.op('dve', f_tt(xn[:, xi, :], acc[:, a, :], lnb[:, :], ALU.add), deps=[tg, st.ln_tok, st.xn_free[xi]])
                item = (xi, txn)
                return None, item
            else:
                tb = S.op('dve', f_tt(acc[:, a, :], acc[:, a, :], lnb[:, :], ALU.add), deps=[tg, st.ln_tok])
            if after is not None:
                after(i, a, tb)
            st.ln_free = tb
            st.accf[a] = tb
            return tb, item

        def store_blk(dst, r0, a, dep):
            tok = S.dma('pool', 'st%d' % a, f_dma(dst[r0:r0 + 128, :], acc[:, a, :]), [dep])
            st.st_tok[a] = tok
            return tok

        def ffn_gateup(l, N, xT_ready, hook=None):
            tH = None
            for j in range(FC):
                if j == min(4, FC - 1) and hook is not None:
                    hook()
                conv_paced(j, tH)
                jj = j % 2
                bg, fg = banks.get()
                bu, fu = banks.get()
                slot, ltok, n = use_unit(('g', l, j))
                t = None
                for kc in range(KC):
                    t = mm(psf[:, bg, 0:N], ring[:, slot, kc * 128:(kc + 1) * 128], xT[:, kc, 0:N], kc == 0, kc == KC - 1,
                           deps=([ltok] + fg + (list(xT_ready) if j == 0 else [])) if kc == 0 else (), inc=(kc == KC - 1))
                st.rel[n] = t
                tokG = t
                slot, ltok, n = use_unit(('u', l, j))
                for kc in range(KC):
                    t = mm(psf[:, bu, 0:N], ring[:, slot, kc * 128:(kc + 1) * 128], xT[:, kc, 0:N], kc == 0, kc == KC - 1,
                           deps=([ltok] + fu) if kc == 0 else (), inc=(kc == KC - 1))
                st.rel[n] = t
                tokU = t
                tS = S.op('act', f_act(sg[:, jj, 0:N], psf[:, bg, 0:N], AF.Silu), deps=[tokG, st.sg_free[jj]])
                tH = S.op('dve', f_stt(hT[:, j, 0:N], psf[:, bu, 0:N], 0.5, sg[:, jj, 0:N], ALU.mult, ALU.mult), deps=[tokU, tS])
                st.sg_free[jj] = tH
                banks.release(bg, [tS])
                banks.release(bu, [tH])
            return tH

        def proj_down(keyf, nkg, nktot, lhs, ablks, lhs_ready, after=None, want_xn=False, acc_ready=None, last_hook=None):
            nb = len(ablks)
            outs = []
            items = []
            xr = [None]

            def evac(t, c, b, tok):
                a = ablks[t]
                te = S.op('dve', f_stt(acc[:, a, c * 512:(c + 1) * 512], acc[:, a, c * 512:(c + 1) * 512], ALPHA,
                                       psf[:, b, 0:512], ALU.mult, ALU.add),
                          deps=[tok] + ([acc_ready[t]] if (acc_ready is not None and c == 0) else []))
                banks.release(b, [te])
                ts_ = S.op('dve', (lambda e, t=t, c=c, a=a: e.bn_stats(stt[:, t, c * 6:(c + 1) * 6], acc[:, a, c * 512:(c + 1) * 512])), deps=[te])
                return ts_

            first = True
            for c in range(3):
                bs = [banks.get() for _ in range(nb)]
                k = 0
                tok = None
                for kg in range(nkg):
                    slot, ltok, n = use_unit(keyf(c, kg))
                    nk = min(4, nktot - 4 * kg)
                    for kk in range(nk):
                        for t in range(nb):
                            deps = []
                            if kk == 0 and t == 0:
                                deps.append(ltok)
                            if k == 0:
                                deps += bs[t][1]
                                if first:
                                    deps += list(lhs_ready)
                                    first = False
                            last = (k == nktot - 1)
                            tok = mm(psf[:, bs[t][0], 0:512], lhs[:, k, t * 128:(t + 1) * 128],
                                     ring[:, slot, kk * 512:(kk + 1) * 512], k == 0, last, deps,
                                     inc=(t == nb - 1 and (last or kk == nk - 1)))
                        k += 1
                    st.rel[n] = tok
                for t in range(nb):
                    evac(t, c, bs[t][0], tok)
            c = 3
            for t in range(nb):
                b, fb = banks.get()
                k = 0
                tok = None
                for kg in range(nkg):
                    slot, ltok, n = use_unit(keyf(c, kg))
                    nk = min(4, nktot - 4 * kg)
                    for kk in range(nk):
                        deps = []
                        if kk == 0:
                            deps.append(ltok)
                        if k == 0:
                            deps += fb
                        last = (k == nktot - 1)
                        tok = mm(psf[:, b, 0:512], lhs[:, k, t * 128:(t + 1) * 128], ring[:, slot, kk * 512:(kk + 1) * 512],
                                 k == 0, last, deps, inc=(last or kk == nk - 1))
                        k += 1
                    st.rel[n] = tok
                ts_ = evac(t, c, b, tok)
                if last_hook is not None and t == nb - 1:
                    last_hook()
                tb, item = ln_block(t, ablks[t], [ts_], after, want_xn)
                outs.append(tb)
                items.append(item)
                if want_xn and t >= 2:
                    xr[0] = transpose_blk(t - 2, items[t - 2])
            if want_xn:
                for t in range(max(0, nb - 2), nb):
                    xr[0] = transpose_blk(t, items[t])
                outs = []
                tlast_xn = items[-1][1]
                for t in range(nb):
                    a = ablks[t]
                    tb = S.op('pool', f_tt(acc[:, a, :], acc[:, a, :], lnb[:, :], ALU.add), deps=[items[t][1], tlast_xn, st.ln_tok])
                    if after is not None:
                        after(t, a, tb)
                    st.ln_free = tb
                    st.accf[a] = tb
                    outs.append(tb)
                return outs, [xr[0]]
            return outs, None

        def kvu(ablks, runs, N, xT_ready, halo):
            nb = len(ablks)
            for g in range(2):
                b, fb = banks.get()
                slot, ltok, n = use_unit(('k', g))
                t = None
                for kc in range(KC):
                    t = mm(psf[:, b, 0:N], ring[:, slot, kc * 128:(kc + 1) * 128], xT[:, kc, 0:N], kc == 0, kc == KC - 1,
                           deps=([ltok] + fb + (list(xT_ready) if g == 0 else [])) if kc == 0 else (), inc=(kc == KC - 1))
                st.rel[n] = t
                te = None
                for (c0, ncol, db) in runs:
                    te = S.op('act', (lambda e, g=g, b=b, c0=c0, ncol=ncol, db=db: e.copy(out=KT[:, g, db * 128:db * 128 + ncol], in_=psf[:, b, c0:c0 + ncol])), deps=[t])
                banks.release(b, [te])
            bs = [banks.get() for _ in range(nb)]
            tok = None
            for i in range(2):
                slot, ltok, n = use_unit(('v', i))
                for kk in range(8):
                    kc = i * 8 + kk
                    for t in range(nb):
                        deps = []
                        if kk == 0 and t == 0:
                            deps.append(ltok)
                        if kc == 0:
                            deps += bs[t][1]
                        tok = mm(psf[:, bs[t][0], 0:256], xT[:, kc, t * 128:(t + 1) * 128], ring[:, slot, kk * 256:(kk + 1) * 256],
                                 kc == 0, kc == KC - 1, deps, inc=(t == nb - 1 and kk == 7))
                st.rel[n] = tok
            blkmap = []
            for (c0, ncol, db) in runs:
                for q in range(ncol // 128):
                    blkmap.append(db + q)
            for t in range(nb):
                te = S.op('dve', f_copy(V[:, blkmap[t], :], psf[:, bs[t][0], 0:256]), deps=[tok])
                banks.release(bs[t][0], [te])
            for c in range(4):
                jj = c % 2
                ba, fa = banks.get()
                bb, fbb = banks.get()
                slot, ltok, n = use_unit(('cc', c))
                t = None
                for kc in range(KC):
                    t = mm(psf[:, ba, 0:N], ring[:, slot, kc * 128:(kc + 1) * 128], xT[:, kc, 0:N], kc == 0, kc == KC - 1,
                           deps=([ltok] + fa) if kc == 0 else (), inc=(kc == KC - 1))
                st.rel[n] = t
                tA = t
                slot, ltok, n = use_unit(('ch', c))
                for kc in range(KC):
                    t = mm(psf[:, bb, 0:N], ring[:, slot, kc * 128:(kc + 1) * 128], xT[:, kc, 0:N], kc == 0, kc == KC - 1,
                           deps=([ltok] + fbb) if kc == 0 else (), inc=(kc == KC - 1))
                st.rel[n] = t
                tB = t
                tS = S.op('act', (lambda e, jj=jj, ba=ba: e.copy(out=sg[:, jj, 0:N], in_=psf[:, ba, 0:N])), deps=[tA, st.sg_free[jj]])
                if halo:
                    t1 = S.op('dve', f_stt(U[:, c, 0:1], psf[:, bb, 127:128], edg[:, 0:1], sg[:, jj, 127:128], ALU.mult, ALU.mult), deps=[tB, tS])
                    t1 = S.op('dve', f_stt(U[:, c, SEQT + 1:SEQT + 2], psf[:, bb, 128:129], edg[:, 1:2], sg[:, jj, 128:129], ALU.mult, ALU.mult), deps=[tB, tS])
                else:
                    (c0, ncol, db) = runs[0]
                    u0 = 1 + (db - 1) * 128
                    t1 = S.op('dve', f_tt(U[:, c, u0:u0 + N], psf[:, bb, 0:N], sg[:, jj, 0:N], ALU.mult), deps=[tB, tS])
                st.sg_free[jj] = t1
                banks.release(ba, [tS])
                banks.release(bb, [t1])

        def ffn_phase(l, ablks, N, li, xT_ready, after=None, want_xn=False, acc_ready=None, last_hook=None):
            tH = ffn_gateup(l, N, xT_ready, lambda: load_ln(li))
            return proj_down(lambda c, kg: ('dn', l, c, kg), NKG, FC, hT, ablks, [tH], after, want_xn, acc_ready, last_hook)

        def prefetch(nxt):
            kind = nxt[0]
            if kind == 'ffn1':
                s_, g_ = nxt[1], nxt[2]
                st.pref = [xn_from_dram(xs, s_ * SEQT + g_ * 512 + q * 128, []) for q in range(min(4, NXN))]
            elif kind == 'mix':
                g_ = nxt[2]
                st.pref = [xn_from_dram(x1scr, g_ * 512 + q * 128, [t for t in st.st_tok.values()]) for q in range(min(4, NXN))]

        def early_input(nxt):
            if nxt[0] == 'ffn1':
                src, row0, deps = xs, nxt[1] * SEQT + nxt[2] * 512, []
            else:
                src, row0, deps = x1scr, nxt[2] * 512, [t for t in st.st_tok.values()]
            items = st.pref if st.pref is not None else []
            st.pref = None
            st.pre_xr = make_xT_items(items, src, row0, deps, 4)

        def ffn1_phase(src, r0, ablks, runs, halo, x1_r0, ld=None, nxt=None):
            nb = len(ablks)
            N = nb * 128
            barrier()
            if halo:
                conv_slice(True)
                ld = load_acc(src, r0, ablks[0], nb)
                xr = make_xT(ablks, ld)
            else:
                if st.pre_xr is not None:
                    xr = st.pre_xr
                    st.pre_xr = None
                else:
                    items = st.pref if st.pref is not None else []
                    st.pref = None
                    xr = make_xT_items(items, src, r0, [], nb, ld)
                if ld is None:
                    ld = load_acc(src, r0, ablks[0], nb)
                conv_slice()
            after = None
            if not halo:
                after = lambda i, a, tb: store_blk(x1scr, x1_r0 + i * 128, a, tb)
            outs, xr = ffn_phase(1, ablks, N, 0, xr, after, True, ld)
            if nxt is not None:
                prefetch(nxt)
            kvu(ablks, runs, N, xr, halo)

        def mem_prep(s, ld=None):
            barrier(True)
            if ld is None:
                ld = load_acc(mem, s * NMEM, 0, 2)
            xr = make_xT([0, 1], ld)
            for h in range(4):
                b, fb = banks.get()
                slot, ltok, n = use_unit(('mk', h))
                t = None
                for kc in range(KC):
                    t = mm(psf[:, b, 0:NMEM], ring[:, slot, kc * 128:(kc + 1) * 128], xT[:, kc, 0:NMEM], kc == 0, kc == KC - 1,
                           deps=([ltok] + fb + (xr if h == 0 else [])) if kc == 0 else (), inc=(kc == KC - 1))
                st.rel[n] = t
                te = S.op('act', (lambda e, h=h, b=b: e.copy(out=mKT[:, h, :], in_=psf[:, b, 0:NMEM])), deps=[t])
                banks.release(b, [te])
            bs = [banks.get() for _ in range(2)]
            tok = None
            for i in range(4):
                slot, ltok, n = use_unit(('mv', i))
                for kk in range(4):
                    kc = i * 4 + kk
                    for t in range(2):
                        deps = []
                        if kk == 0 and t == 0:
                            deps.append(ltok)
                        if kc == 0:
                            deps += bs[t][1]
                        tok = mm(psf[:, bs[t][0], 0:512], xT[:, kc, t * 128:(t + 1) * 128], ring[:, slot, kk * 512:(kk + 1) * 512],
                                 kc == 0, kc == KC - 1, deps, inc=(t == 1 and kk == 3))
                st.rel[n] = tok
            for t in range(2):
                te = S.op('dve', f_copy(mV[:, t, :], psf[:, bs[t][0], 0:512]), deps=[tok])
                banks.release(bs[t][0], [te])

        def mix_phase(s, grp, halo_seq, nxt=None):
            ablks = [0, 1, 2, 3]
            t0 = grp * 512
            barrier()
            if st.pre_xr is not None:
                xr = st.pre_xr
                st.pre_xr = None
            else:
                items = st.pref if st.pref is not None else []
                st.pref = None
                xr = make_xT_items(items, x1scr, t0, [t for t in st.st_tok.values()], 4)
            ld = load_acc(x1scr, t0, 0, 4)
            conv_slice(True)
            tcu = None
            for c in range(4):
                tcu = S.op('dve', f_ts(cu[:, c, :], U[:, c, 1 + t0:1 + t0 + 512], cw[:, 3 * c + 1:3 * c + 2], None, ALU.mult))
                tcu = S.op('dve', f_stt(cu[:, c, :], U[:, c, t0:t0 + 512], cw[:, 3 * c:3 * c + 1], cu[:, c, :], ALU.mult, ALU.add), deps=[tcu])
                tcu = S.op('dve', f_stt(cu[:, c, :], U[:, c, 2 + t0:2 + t0 + 512], cw[:, 3 * c + 2:3 * c + 3], cu[:, c, :], ALU.mult, ALU.add), deps=[tcu])
            first = True
            tq = None
            for kind, cnt in (('q', 8), ('qx', 4), ('cb', 4)):
                for h in range(cnt):
                    b, fb = banks.get()
                    slot, ltok, n = use_unit((kind, h))
                    t = None
                    for kc in range(KC):
                        t = mm(psf[:, b, :], ring[:, slot, kc * 128:(kc + 1) * 128], xT[:, kc, :], kc == 0, kc == KC - 1,
                               deps=([ltok] + fb + (xr if first else [])) if kc == 0 else (), inc=(kc == KC - 1))
                    first = False
                    st.rel[n] = t
                    if kind == 'q':
                        te = S.op('act', (lambda e, h=h, b=b: e.copy(out=qT[:, h, :], in_=psf[:, b, :])), deps=[t])
                        tq = te
                    elif kind == 'qx':
                        te = S.op('act', (lambda e, h=h, b=b: e.copy(out=qxT[:, h, :], in_=psf[:, b, :])), deps=[t])
                        tqx = te
                    else:
                        te = S.op('dve', f_tt(ycat[:, 8 + h, :], psf[:, b, :], cu[:, h, :], ALU.mult), deps=[t, tcu])
                    banks.release(b, [te])
            pt_free = [None, None, None]
            stmp_free = [None, None]
            sgb = sg[:, :, :].rearrange("p a b -> p (a b)").bitcast(BF16).rearrange("p (a b) -> p a b", b=512)

            def PTs(un, jj):
                if un < 2:
                    return PT[:, un * 3 + jj, :]
                return sgb[:, jj, :]
            units = [('w', b, g) for b in range(4) for g in range(2)] + [('m', h, 0) for h in range(4)]
            sc = {'si': 0}
            tl = {'t': None}

            def score_stage(ui, unit):
                un = ui % 3
                extra = list(st.sg_free) if un == 2 else []
                kind = unit[0]
                pts = []
                if kind == 'w':
                    b, g = unit[1], unit[2]
                    bidx = 1 + 4 * grp + b
                    kbs = [(jj, bidx - 1 + jj) for jj in range(3)]
                    kbs = [(jj, kb) for jj, kb in kbs if halo_seq or (1 <= kb <= NBS)]
                    for jj, kb in kbs:
                        bsx, fs = banks.get()
                        ps3 = psf[:, bsx, :].rearrange("p (h t) -> p h t", h=4)
                        mm(ps3, KT[:, g, kb * 128:(kb + 1) * 128], qT[:, 4 * g:4 * g + 4, b * 128:(b + 1) * 128],
                           True, False, deps=fs + [tq], inc=False)
                        mm(ps3, ident[:, :], bhl[:, 0, jj, 4 * g:4 * g + 4, :], False, False, inc=False)
                        tqk = mm(ps3, ident[:, :], bhl[:, 1, jj, 4 * g:4 * g + 4, :], False, True, inc=True)
                        eb = None
                        if kb == 0:
                            eb = edg[:, 2:3]
                        elif kb == NBS + 1:
                            eb = edg[:, 3:4]
                        tp = S.op('act', f_act(PTs(un, jj), psf[:, bsx, :], AF.Exp, bias=eb, scale=SCALE),
                                  deps=[tqk, pt_free[un]] + extra)
                        banks.release(bsx, [tp])
                        pts.append((jj, kb, tp))
                else:
                    h = unit[1]
                    for j in range(2):
                        bsx, fs = banks.get()
                        tqk = mm(psf[:, bsx, :], mKT[:, h, j * 128:(j + 1) * 128], qxT[:, h, :], True, True, deps=fs + [tqx], inc=True)
                        tp = S.op('act', f_act(PTs(un, j), psf[:, bsx, :], AF.Exp, scale=SCALE), deps=[tqk, pt_free[un]] + extra)
                        banks.release(bsx, [tp])
                        pts.append((j, j, tp))
                return pts

            def pv_stage(ui, unit, pts):
                un = ui % 3
                u2 = ui % 2
                kind = unit[0]
                bo, fo = banks.get()
                bd, fd = banks.get()
                nk = len(pts)
                for i, (jj, kb, tp) in enumerate(pts):
                    if kind == 'w':
                        lhs = V[:, kb, unit[2] * 128:(unit[2] + 1) * 128]
                    else:
                        lhs = mV[:, jj, unit[1] * 128:(unit[1] + 1) * 128]
                    mm(psf[:, bo, :], lhs, PTs(un, jj), i == 0, i == nk - 1, deps=[tp] + (fo if i == 0 else []), inc=False)
                to = None
                for i, (jj, kb, tp) in enumerate(pts):
                    to = mm(psf[:, bd, :], ones[:, :], PTs(un, jj), i == 0, i == nk - 1,
                            deps=(fd if i == 0 else []), inc=(i == nk - 1))
                pt_free[un] = to
                if kind == 'w':
                    b, g = unit[1], unit[2]
                    d3 = dn[:, u2, :].rearrange("p (h t) -> p h t", h=4)
                    t1 = S.op('dve', f_tt(d3, psf[:, bd, :].rearrange("p (h t) -> p h t", h=4),
                                          esk[:, 4 * g:4 * g + 4].unsqueeze(2).to_broadcast([128, 4, 128]), ALU.add), deps=[to])
                    t2 = S.op('dve', (lambda e, un=u2: e.reciprocal(dn[:, un, :], dn[:, un, :])), deps=[t1])
                    t3 = S.op('dve', f_tt(ycat[:, 4 * g:4 * g + 4, b * 128:(b + 1) * 128],
                                          psf[:, bo, :].rearrange("p (h t) -> p h t", h=4), d3, ALU.mult), deps=[t2])
                    banks.release(bd, [t1])
                else:
                    h = unit[1]
                    t2 = S.op('dve', (lambda e, un=u2, bd=bd: e.reciprocal(dn[:, un, :], psf[:, bd, :])), deps=[to])
                    t3 = S.op('dve', f_tt(ycat[:, 12 + h, :], psf[:, bo, :], dn[:, u2, :], ALU.mult), deps=[t2])
                    banks.release(bd, [t2])
                banks.release(bo, [t3])
                tl['t'] = t3

            pend = []
            for ui, unit in enumerate(units):
                pts = score_stage(ui, unit)
                pend.append((ui, unit, pts))
                if len(pend) > 2:
                    pv_stage(*pend.pop(0))
            while pend:
                pv_stage(*pend.pop(0))
            tlast = tl['t']
            st.sg_free = [pt_free[2], pt_free[2]]
            load_ln(1)
            gi = 0
            tM = None
            for m in range(16):
                slotB, ltB, nB = use_unit(('wb', m))
                mi = m % 2
                for br, (k0, k1) in enumerate(((0, 8), (8, 12), (12, 16))):
                    bz, fz = banks.get()
                    tz = None
                    for kc in range(k0, k1):
                        tz = mm(psf[:, bz, :], ring[:, slotB, kc * 128:(kc + 1) * 128], ycat[:, kc, :], kc == k0, kc == k1 - 1,
                                deps=(fz + [ltB, tlast]) if kc == k0 else (), inc=(kc == k1 - 1))
                    if br == 2:
                        st.rel[nB] = tz
                    slotG, ltG, nG = use_unit(('wg', m, br))
                    bgx, fg = banks.get()
                    tg = None
                    for kc in range(KC):
                        tg = mm(psf[:, bgx, :], ring[:, slotG, kc * 128:(kc + 1) * 128], xT[:, kc, :], kc == 0, kc == KC - 1,
                                deps=(fg + [ltG]) if kc == 0 else (), inc=(kc == KC - 1))
                    st.rel[nG] = tg
                    sx = gi % 2
                    gi += 1
                    tS = S.op('act', f_act(sgt[:, sx, :], psf[:, bgx, :], AF.Sigmoid, bias=bgt[:, br * 16 + m:br * 16 + m + 1]),
                              deps=[tg, tM])
                    if br == 0:
                        tM = S.op('dve', f_tt(tac[:, mi, :], psf[:, bz, :], sgt[:, sx, :], ALU.mult), deps=[tz, tS])
                        tzr = tM
                    elif br == 1:
                        tzr = S.op('dve', f_tt(tm2[:, mi, :], psf[:, bz, :], sgt[:, sx, :], ALU.mult), deps=[tz, tS])
                        tM = S.op('dve', f_tt(tac[:, mi, :], tac[:, mi, :], tm2[:, mi, :], ALU.add), deps=[tzr])
                    else:
                        tzr = S.op('dve', f_tt(tm2[:, mi, :], psf[:, bz, :], sgt[:, sx, :], ALU.mult), deps=[tz, tS])
                        tM = S.op('dve', f_tt(merged[:, m, :], tac[:, mi, :], tm2[:, mi, :], ALU.add), deps=[tzr])
                    banks.release(bz, [tzr])
                    banks.release(bgx, [tS])
            outs, xr = proj_down(lambda c, kg: ('wo', c, kg), 4, 16, merged, ablks, [tM], None, True, ld)
            if nxt is not None:
                prefetch(nxt)
            after = lambda i, a, tb: store_blk(ys, s * SEQT + t0 + i * 128, a, tb)
            outs, _ = ffn_phase(2, ablks, 512, 2, xr, after, False, outs,
                                (lambda: early_input(nxt)) if nxt is not None else None)

        def setup():
            c = []
            c.append(S.dma('sp', 'cs', f_dma(tab[0:32, :], table[:, :])))
            c.append(S.dma('sp', 'cs', f_dma(ohs[:, :], oh[:, :])))
            c.append(S.dma('sp', 'cs', f_dma(idf[:, :], identin[:, :])))
            c.append(S.dma('sp', 'cs', f_dma(bgt[:, :], bgate[:, :])))
            c.append(S.dma('sp', 'cs', f_dma(cw[:, :], convw[:, :])))
            c.append(S.dma('sp', 'cs', f_dma(edg[:, :], edge[:, :])))
            c.append(S.dma('sp', 'cs', f_dma(esk[:, :], bass.AP(sink.tensor, 0, [[0, 128], [1, 8]]))))
            cs = c[-1]
            t = S.op('dve', f_memset(tab[32:33, :], 1.0))
            t = S.op('dve', f_memset(ones[:, :], 1.0))
            t = S.op('dve', f_memset(U[:, :, 0:1], 0.0))
            t = S.op('dve', f_memset(U[:, :, SEQT + 1:SEQT + 2], 0.0))
            if not cfg.HALO:
                pass
            ti = S.op('act', (lambda e: e.copy(out=ident[:, :], in_=idf[:, :])), deps=[cs])
            te = S.op('act', f_act(esk[:, :], esk[:, :], AF.Exp), deps=[cs])
            tg = None
            for blk in range(3):
                b, fb = banks.get()
                tmm = mm(psf[0:8, b, 0:255], tab[:, :], ohs[:, blk * 255:(blk + 1) * 255], True, True, deps=[cs, t] + fb, inc=True)
                tg = S.op('dve', f_copy(gv[:, blk * 255:(blk + 1) * 255], psf[0:8, b, 0:255]), deps=[tmm])
                banks.release(b, [tg])
            d1 = S.dma('sp', 'cs', f_dma(bass.AP(gsc.tensor, 0, [[255, 8], [8 * 255, 3], [1, 255]]),
                                         gv[:, :].rearrange("p (b s) -> p b s", s=255)), deps=[tg])
            d2 = S.dma('sp', 'cs', f_dma(bass.AP(zsc.tensor, 0, [[128 * 255, 24], [255, 128], [1, 255]]),
                                         bass.AP(gsc.tensor, 0, [[255, 24], [0, 128], [1, 255]])), deps=[d1])
            btmp = work[:, 8192:8192 + 6144].bitcast(F32)
            btmp2 = work[:, 14336:14336 + 6144].bitcast(F32)
            d3 = S.dma('sp', 'cs', f_dma(btmp.rearrange("p (i t) -> p i t", t=128),
                                         bass.AP(zsc.tensor, 0, [[254, 128], [128 * 255, 24], [1, 128]])), deps=[d2])
            bhi = bhl[:, 0, :, :, :].rearrange("p b h t -> p (b h t)")
            blo = bhl[:, 1, :, :, :].rearrange("p b h t -> p (b h t)")
            tb_ = S.op('dve', f_ts(btmp, btmp, 1.0 / SCALE, None, ALU.mult), deps=[d3])
            tb_ = S.op('dve', f_copy(bhi, btmp), deps=[tb_])
            tb_ = S.op('dve', f_tt(btmp2, btmp, bhi, ALU.subtract), deps=[tb_])
            tb_ = S.op('dve', f_copy(blo, btmp2), deps=[tb_])
            for e in ('pe', 'act', 'dve', 'pool'):
                S.wait(e, [d3, ti, te, t, tb_])

        setup()
        ld_mem0 = load_acc(mem, 0, 0, 2)
        c_split = (offs[('q', 0)][0] + CW - 1) // CW
        c_mem = (offs[('g', 1, 0)][0] + CW - 1) // CW
        conversions(0, c_mem)
        st.ldx = None
        st.conv_rest = (c_split, NCH)
        st.conv_per = (NCH - c_split + NG - 1) // NG
        phases = []
        for s in range(NSEQ):
            halo_seq = (s == 0 and cfg.HALO)
            phases.append(('mem', s))
            for g in range(NG):
                phases.append(('ffn1', s, g))
            if halo_seq:
                phases.append(('halo',))
            for g in range(NG):
                phases.append(('mix', s, g, halo_seq))
        for pi, ph in enumerate(phases):
            nxt = phases[pi + 1] if pi + 1 < len(phases) else None
            if nxt is not None and nxt[0] not in ('ffn1', 'mix'):
                nxt = None
            kind = ph[0]
            if kind == 'mem':
                s = ph[1]
                if s == 1 and cfg.HALO:
                    S.op('dve', f_memset(U[:, :, 0:1], 0.0))
                    S.op('dve', f_memset(U[:, :, SEQT + 1:SEQT + 2], 0.0))
                mem_prep(s, ld_mem0 if s == 0 else None)
                if s == 0:
                    st.pref = [xn_from_dram(xs, q * 128, []) for q in range(min(4, NXN))]
                    st.ldx = load_acc(xs, 0, 0, 4)
                    conversions(c_mem, c_split)
            elif kind == 'ffn1':
                s, g = ph[1], ph[2]
                ldx = st.ldx
                st.ldx = None
                ffn1_phase(xs, s * SEQT + g * 512, [0, 1, 2, 3], [(0, 512, 1 + 4 * g)], False, g * 512, ldx, nxt)
            elif kind == 'halo':
                ffn1_phase(xh, 0, [2, 3], [(0, 128, 0), (128, 128, NBS + 1)], True, 0, None, nxt)
            else:
                mix_phase(ph[1], ph[2], ph[3], nxt)
        S.wait('pool', list(st.st_tok.values()))

        semnames = set()
        for e in ENG:
            for o in S.ops[e]:
                if o[0] == 'w':
                    semnames.add(o[1])
                elif o[0] == 'i':
                    semnames.add(o[2])
        sems = {k: es.enter_context(nc.semaphore("s_" + k)) for k in sorted(semnames)}
        block = es.enter_context(nc.Block())

        def emit(eng_obj, ops):
            for o in ops:
                if o[0] == 'w':
                    eng_obj.wait_ge(sems[o[1]], o[2])
                elif o[0] == 'i':
                    o[1](eng_obj).then_inc(sems[o[2]], o[3])
                else:
                    o[1](eng_obj)

        @block.tensor
        def _(e):
            emit(e, S.ops['pe'])

        @block.scalar
        def _(e):
            emit(e, S.ops['act'])

        @block.vector
        def _(e):
            emit(e, S.ops['dve'])

        @block.gpsimd
        def _(e):
            emit(e, S.ops['pool'])

        @block.sync
        def _(e):
            emit(e, S.ops['sp'])
    return nc


def host_consts(rel_bias_table, ln1_g, ln1_b, ln2_g, ln2_b, ln3_g, ln3_b, b_gate, conv_w, attn_sink):
    lnp = np.stack([ln1_g[0], ln1_b[0], ln2_g[0], ln2_b[0], ln3_g[0], ln3_b[0]]).astype(np.float32)
    bg = np.ascontiguousarray(b_gate[0].reshape(3, 16, 128).transpose(2, 0, 1)).reshape(128, 48)
    cwv = np.ascontiguousarray(conv_w[0].reshape(3, 4, 128).transpose(2, 1, 0)).reshape(128, 12)
    return {
        "lnp": np.ascontiguousarray(lnp), "bgate": np.ascontiguousarray(bg), "convw": np.ascontiguousarray(cwv),
        "sink": np.ascontiguousarray(attn_sink.reshape(1, 8)), "table": np.ascontiguousarray(rel_bias_table),
        "oh": bias_onehot(), "identin": np.eye(128, dtype=np.float32),
    }


def kernel(x_prompt, x_sample, mem_prompt, mem_sample, rel_bias_table, ln1_g, ln1_b, ffn1_w_gate_up,
           ffn1_w_down, w_in, conv_w, w_mem_kv, attn_sink, w_gate, b_gate, w_branch, w_o, ln2_g, ln2_b,
           ffn2_w_gate_up, ffn2_w_down, ln3_g, ln3_b):
    f = lambda a: np.asarray(a, dtype=np.float32)
    x_prompt, x_sample, mem_prompt, mem_sample = f(x_prompt), f(x_sample), f(mem_prompt), f(mem_sample)
    cfg = Cfg()
    wflat = pack_weights(cfg, f(ffn1_w_gate_up)[0], f(ffn1_w_down)[0], f(w_in)[0], f(w_mem_kv)[0], f(w_gate)[0],
                         f(w_branch)[0], f(w_o)[0], f(ffn2_w_gate_up)[0], f(ffn2_w_down)[0])
    consts = host_consts(f(rel_bias_table), f(ln1_g), f(ln1_b), f(ln2_g), f(ln2_b), f(ln3_g), f(ln3_b),
                         f(b_gate), f(conv_w), f(attn_sink))
    nc = build_program(cfg)
    in_maps = []
    CH = 2048
    for i in range(8):
        sb_, ch = i // 4, i % 4
        xs = np.concatenate([x_sample[sb_, ch * CH:(ch + 1) * CH], x_prompt[2 * i], x_prompt[2 * i + 1]], axis=0)
        xh = np.zeros((256, D), np.float32)
        fl, fr = 0.0, 0.0
        if ch > 0:
            xh[0:128] = x_sample[sb_, ch * CH - 128:ch * CH]
            fl = 1.0
        if ch < 3:
            xh[128:256] = x_sample[sb_, (ch + 1) * CH:(ch + 1) * CH + 128]
            fr = 1.0
        mem = np.concatenate([mem_sample[sb_], mem_prompt[2 * i], mem_prompt[2 * i + 1]], axis=0)
        edge = np.tile(np.array([[fl, fr, 0.0 if fl else NEG, 0.0 if fr else NEG]], np.float32), (128, 1))
        m = {"xs": np.ascontiguousarray(xs), "xh": xh, "mem": np.ascontiguousarray(mem), "wflat": wflat, "edge": edge}
        m.update(consts)
        in_maps.append(m)
    res = run_bass_kernel_spmd(nc, in_maps, core_ids=list(range(8)))
    y_prompt = np.zeros((16, 2048, D), np.float32)
    y_sample = np.zeros((2, 8192, D), np.float32)
    for i in range(8):
        ysd = np.asarray(res.results[i]["ys"], dtype=np.float32)
        sb_, ch = i // 4, i % 4
        y_sample[sb_, ch * CH:(ch + 1) * CH] = ysd[0:CH]
        y_prompt[2 * i] = ysd[CH:2 * CH]
        y_prompt[2 * i + 1] = ysd[2 * CH:3 * CH]
    return (y_prompt, y_sample)
```
